# Optimizing a Trainium2 kernel written in Bass

```python
import math
import jax, jax.numpy as jnp
from jax import lax
import numpy as np

D_MODEL = 4096
BATCH = 4
SEQ = 4096
DEPTH = 4

HEAD_DIM = 128
BLOCK = 128
ROPE_THETA = 10000.0
EPS = 1e-6
SWA_Q_HEADS = 16
SWA_KV_HEADS = 4
WINDOW = 128
N_BAND = -(-WINDOW // BLOCK) + 1
SB_HEADS = 8
DIFF_HEADS = 8
DIFF_QK_DIM = HEAD_DIM // 2
DIFF_NORM_EPS = 1e-5
D_FF = 4096
N_BRANCH = 3

SWA_Q_W = SWA_Q_HEADS * HEAD_DIM
SWA_KV_W = SWA_KV_HEADS * HEAD_DIM
SB_W = SB_HEADS * HEAD_DIM
DIFF_W = DIFF_HEADS * HEAD_DIM
QKV_WIDTHS = (SWA_Q_W, SWA_KV_W, SWA_KV_W, SB_W, SB_W, SB_W, DIFF_W, DIFF_W, DIFF_W)
QKV_W = sum(QKV_WIDTHS)

kernel_name = "hybrid_gated_swa_stickbreak_diffattn_macaron"


def rms_norm(x, g, eps=EPS):
    xf = x.astype(jnp.float32)
    y = xf * lax.rsqrt(jnp.mean(xf * xf, axis=-1, keepdims=True) + eps)
    return (y * g.astype(jnp.float32)).astype(x.dtype)


def swiglu(x, w_in, w_out):
    gu = x @ w_in
    g, u = jnp.split(gu, 2, axis=-1)
    return (jax.nn.silu(g) * u) @ w_out


def rope_tables(seq, dim):
    inv = 1.0 / (ROPE_THETA ** (jnp.arange(0, dim, 2, dtype=jnp.float32) / dim))
    ang = jnp.arange(seq, dtype=jnp.float32)[:, None] * inv[None, :]
    return jnp.cos(ang), jnp.sin(ang)


def apply_rope(x, cos, sin):
    xf = x.astype(jnp.float32)
    half = xf.shape[-1] // 2
    x1, x2 = xf[..., :half], xf[..., half:]
    c, s = cos[:, None, :], sin[:, None, :]
    return jnp.concatenate([x1 * c - x2 * s, x2 * c + x1 * s], axis=-1).astype(x.dtype)


def sliding_window_gqa(q, k, v, sinks):
    b, s, hq, d = q.shape
    hkv = k.shape[2]
    g = hq // hkv
    nb = s // BLOCK
    qb = q.reshape(b, nb, BLOCK, hkv, g, d)

    def band(t):
        tb = t.reshape(b, nb, BLOCK, hkv, d)
        tp = jnp.pad(tb, ((0, 0), (N_BAND - 1, 0), (0, 0), (0, 0), (0, 0)))
        return jnp.concatenate([tp[:, j:j + nb] for j in range(N_BAND)], axis=2)

    kk, vv = band(k), band(v)
    scores = jnp.einsum('bnqhgd,bnkhd->bnhgqk', qb, kk,
                        preferred_element_type=jnp.float32) * (1.0 / math.sqrt(d))
    qi = jnp.arange(BLOCK)[:, None]
    kj = jnp.arange(N_BAND * BLOCK)[None, :]
    rel = qi - kj + (N_BAND - 1) * BLOCK
    key_pos = (jnp.arange(nb)[:, None, None] - (N_BAND - 1)) * BLOCK + kj[None]
    valid = (rel >= 0)[None] & (rel < WINDOW)[None] & (key_pos >= 0)
    scores = jnp.where(valid[None, :, None, None], scores, -jnp.inf)
    sink = jnp.broadcast_to(sinks.astype(jnp.float32).reshape(1, 1, hkv, g, 1, 1),
                            scores.shape[:-1] + (1,))
    probs = jax.nn.softmax(jnp.concatenate([scores, sink], axis=-1), axis=-1)[..., :-1]
    out = jnp.einsum('bnhgqk,bnkhd->bnqhgd', probs.astype(v.dtype), vv)
    return out.reshape(b, s, hq * d)


def stick_breaking_attention(q, k, v):
    b, s, h, d = q.shape
    nb = s // BLOCK
    qb = q.reshape(b, nb, BLOCK, h, d).transpose(1, 0, 2, 3, 4)
    key_idx = jnp.arange(s)

    def one_block(args):
        i, qblk = args
        z = jnp.einsum('bqhd,bkhd->bhqk', qblk, k,
                       preferred_element_type=jnp.float32) * (1.0 / math.sqrt(d))
        t = i * BLOCK + jnp.arange(BLOCK)
        causal = key_idx[None, :] < t[:, None]
        log_beta = jax.nn.log_sigmoid(z)
        log_1m = jnp.where(causal, jax.nn.log_sigmoid(-z), 0.0)
        after = lax.cumsum(log_1m, axis=3, reverse=True) - log_1m
        a = jnp.where(causal, jnp.exp(log_beta + after), 0.0)
        return jnp.einsum('bhqk,bkhd->bqhd', a.astype(v.dtype), v)

    out = lax.map(one_block, (jnp.arange(nb), qb))
    return out.transpose(1, 0, 2, 3, 4).reshape(b, s, h * d)


def differential_attention(q, k, v, lam):
    b, s, h, _, dh = q.shape
    nb = s // BLOCK
    qb = q.reshape(b, nb, BLOCK, h, 2, dh).transpose(1, 0, 2, 3, 4, 5)
    key_idx = jnp.arange(s)

    def one_block(args):
        i, qblk = args
        sc = jnp.einsum('bqhcd,bkhcd->bhcqk', qblk, k,
                        preferred_element_type=jnp.float32) * (1.0 / math.sqrt(dh))
        t = i * BLOCK + jnp.arange(BLOCK)
        causal = key_idx[None, :] <= t[:, None]
        p = jax.nn.softmax(jnp.where(causal, sc, -jnp.inf), axis=-1)
        w = p[:, :, 0] - lam * p[:, :, 1]
        return jnp.einsum('bhqk,bkhd->bqhd', w.astype(v.dtype), v)

    out = lax.map(one_block, (jnp.arange(nb), qb))
    return out.transpose(1, 0, 2, 3, 4).reshape(b, s, h, v.shape[-1])


def setup_inputs(seed: int = 0) -> dict:
    key = jax.random.key(seed)
    ks = iter(jax.random.split(key, 32))
    f32 = jnp.float32

    def w(shape, fan_in):
        return jax.random.normal(next(ks), shape, f32) * (fan_in ** -0.5)

    def gain(shape):
        return 1.0 + 0.02 * jax.random.normal(next(ks), shape, f32)

    return {
        "x": jax.random.normal(next(ks), (BATCH, SEQ, D_MODEL), f32),
        "ffn1_norm": gain((DEPTH, D_MODEL)),
        "ffn1_w_in": w((DEPTH, D_MODEL, 2 * D_FF), D_MODEL),
        "ffn1_w_out": w((DEPTH, D_FF, D_MODEL), D_FF),
        "mix_norm": gain((DEPTH, D_MODEL)),
        "w_qkv": w((DEPTH, D_MODEL, QKV_W), D_MODEL),
        "w_gate": w((DEPTH, D_MODEL, N_BRANCH * D_MODEL), D_MODEL),
        "sinks": 0.5 * jax.random.normal(next(ks), (DEPTH, SWA_Q_HEADS), f32),
        "lambda_q1": 0.1 * jax.random.normal(next(ks), (DEPTH, DIFF_QK_DIM), f32),
        "lambda_k1": 0.1 * jax.random.normal(next(ks), (DEPTH, DIFF_QK_DIM), f32),
        "lambda_q2": 0.1 * jax.random.normal(next(ks), (DEPTH, DIFF_QK_DIM), f32),
        "lambda_k2": 0.1 * jax.random.normal(next(ks), (DEPTH, DIFF_QK_DIM), f32),
        "diff_norm": gain((DEPTH, HEAD_DIM)),
        "w_branch_a": w((DEPTH, SWA_Q_W, D_MODEL), SWA_Q_W),
        "w_branch_b": w((DEPTH, SB_W, D_MODEL), SB_W),
        "w_branch_c": w((DEPTH, DIFF_W, D_MODEL), DIFF_W),
        "w_out": w((DEPTH, D_MODEL, D_MODEL), D_MODEL),
        "ffn2_norm": gain((DEPTH, D_MODEL)),
        "ffn2_w_in": w((DEPTH, D_MODEL, 2 * D_FF), D_MODEL),
        "ffn2_w_out": w((DEPTH, D_FF, D_MODEL), D_FF),
        "final_norm": gain((D_MODEL,)),
    }


def reference(x, ffn1_norm, ffn1_w_in, ffn1_w_out, mix_norm, w_qkv, w_gate, sinks,
              lambda_q1, lambda_k1, lambda_q2, lambda_k2, diff_norm,
              w_branch_a, w_branch_b, w_branch_c, w_out,
              ffn2_norm, ffn2_w_in, ffn2_w_out, final_norm):
    b, s, _ = x.shape
    cos_a, sin_a = rope_tables(s, HEAD_DIM)
    cos_c, sin_c = rope_tables(s, DIFF_QK_DIM)
    split_idx = [int(i) for i in np.cumsum(QKV_WIDTHS)[:-1]]

    for l in range(DEPTH):
        x = x + 0.5 * swiglu(rms_norm(x, ffn1_norm[l]), ffn1_w_in[l], ffn1_w_out[l])

        h = rms_norm(x, mix_norm[l])
        qkv = h @ w_qkv[l]
        qa, ka, va, qb, kb, vb, qc, kc, vc = jnp.split(qkv, split_idx, axis=-1)

        qa = apply_rope(qa.reshape(b, s, SWA_Q_HEADS, HEAD_DIM), cos_a, sin_a)
        ka = apply_rope(ka.reshape(b, s, SWA_KV_HEADS, HEAD_DIM), cos_a, sin_a)
        va = va.reshape(b, s, SWA_KV_HEADS, HEAD_DIM)
        out_a = sliding_window_gqa(qa, ka, va, sinks[l])

        out_b = stick_breaking_attention(qb.reshape(b, s, SB_HEADS, HEAD_DIM),
                                         kb.reshape(b, s, SB_HEADS, HEAD_DIM),
                                         vb.reshape(b, s, SB_HEADS, HEAD_DIM))

        qc = apply_rope(qc.reshape(b, s, DIFF_HEADS * 2, DIFF_QK_DIM), cos_c, sin_c)
        kc = apply_rope(kc.reshape(b, s, DIFF_HEADS * 2, DIFF_QK_DIM), cos_c, sin_c)
        qc = qc.reshape(b, s, DIFF_HEADS, 2, DIFF_QK_DIM)
        kc = kc.reshape(b, s, DIFF_HEADS, 2, DIFF_QK_DIM)
        vc = vc.reshape(b, s, DIFF_HEADS, HEAD_DIM)
        lam_init = 0.8 - 0.6 * math.exp(-0.3 * l)
        lam = (jnp.exp(jnp.sum(lambda_q1[l].astype(jnp.float32) * lambda_k1[l].astype(jnp.float32)))
               - jnp.exp(jnp.sum(lambda_q2[l].astype(jnp.float32) * lambda_k2[l].astype(jnp.float32)))
               + lam_init)
        oc = differential_attention(qc, kc, vc, lam)
        oc = rms_norm(oc, diff_norm[l], DIFF_NORM_EPS) * (1.0 - lam_init)
        out_c = oc.reshape(b, s, DIFF_W).astype(x.dtype)

        gates = jax.nn.sigmoid((h @ w_gate[l]).astype(jnp.float32)).astype(x.dtype)
        g_a, g_b, g_c = jnp.split(gates, N_BRANCH, axis=-1)
        merged = (g_a * (out_a @ w_branch_a[l])
                  + g_b * (out_b @ w_branch_b[l])
                  + g_c * (out_c @ w_branch_c[l]))
        x = x + merged @ w_out[l]

        x = x + 0.5 * swiglu(rms_norm(x, ffn2_norm[l]), ffn2_w_in[l], ffn2_w_out[l])

    return rms_norm(x, final_norm)
```

```python
import math
from contextlib import ExitStack

import numpy as np
import concourse.bass as bass
import concourse.mybir as mybir
from concourse.bass_utils import run_bass_kernel_spmd

F32 = mybir.dt.float32
BF16 = mybir.dt.bfloat16
AF = mybir.ActivationFunctionType
ALU = mybir.AluOpType

ROPE_THETA = 10000.0
EPS = 1e-6
DIFF_EPS = 1e-5


class Cfg:
    def __init__(self, D=4096, S=4096, FF=4096, NL=4, AQ=16, AKV=4, BH=8, CH=8, TP=1024, NW=4, B=4):
        self.D, self.S, self.FF, self.NL = D, S, FF, NL
        self.AQ, self.AKV, self.BH, self.CH = AQ, AKV, BH, CH
        self.TP, self.NW, self.B = TP, NW, B
        self.DC, self.FC = D // 128, FF // 128
        self.G = AQ // AKV
        self.QKC = AQ + 2 * AKV + 3 * BH + 3 * CH
        o = 0
        self.c_qa = o; o += AQ
        self.c_ka = o; o += AKV
        self.c_va = o; o += AKV
        self.c_qb = o; o += BH
        self.c_kb = o; o += BH
        self.c_vb = o; o += BH
        self.c_qc = o; o += CH
        self.c_kc = o; o += CH
        self.c_vc = o; o += CH
        self.r_qa = 0
        self.r_ka = AQ
        self.r_qb = AQ + AKV
        self.r_kb = AQ + AKV + BH
        self.r_qc = AQ + AKV + 2 * BH
        self.r_kc = AQ + AKV + 2 * BH + CH
        self.NQK = AQ + AKV + 2 * BH + 2 * CH
        self.VW = (AKV + BH + CH) * 128
        self.NOC = AQ + BH + CH
        o = 0
        self.v_n1 = o; o += NL * self.DC
        self.v_nm = o; o += NL * self.DC
        self.v_n2 = o; o += NL * self.DC
        self.v_nf = o; o += self.DC
        self.v_sink = o; o += NL * AQ
        self.v_lq1 = o; o += NL * 64
        self.v_lk1 = o; o += NL * 64
        self.v_lq2 = o; o += NL * 64
        self.v_lk2 = o; o += NL * 64
        self.v_dn = o; o += NL
        self.NV = o
        o = 0
        self.k_ones = o; o += 128
        self.k_ident = o; o += 128
        self.k_permA = o; o += 128
        self.k_permC = o; o += 128
        self.k_tri = o; o += 128
        self.k_mAd = o; o += self.G * 128
        self.k_mAp = o; o += self.G * 128
        self.k_mC = o; o += 4 * 512
        self.k_mB = o; o += 4 * 512
        self.NCC = o

    def wspecs(self):
        D, FF = self.D, self.FF
        return [("ffn1_w_in", D, 2 * FF), ("ffn1_w_out", FF, D), ("w_qkv", D, self.QKC * 128),
                ("w_gate", D, 3 * D), ("w_branch_a", self.AQ * 128, D), ("w_branch_b", self.BH * 128, D),
                ("w_branch_c", self.CH * 128, D), ("w_out", D, D), ("ffn2_w_in", D, 2 * FF),
                ("ffn2_w_out", FF, D)]


_UID = [0]


def un(name):
    _UID[0] += 1
    return f"{name}_{_UID[0]}"


class EngW:
    def __init__(self, eng, sem):
        self.eng, self.sem, self.cnt, self.seen = eng, sem, 0, {}


class Res:
    __slots__ = ("w", "r", "dsem")

    def __init__(self, dsem=None):
        self.w = None
        self.r = {}
        self.dsem = dsem


class KB:
    def __init__(self, nc):
        self.nc = nc
        self.top = ExitStack()
        self.nsem = 0
        self.PE = EngW(nc.tensor, self.mksem())
        self.ACT = EngW(nc.scalar, self.mksem())
        self.DVE = EngW(nc.vector, self.mksem())
        self.POOL = EngW(nc.gpsimd, self.mksem())
        self.SP = EngW(nc.sync, None)
        self.engs = [self.PE, self.ACT, self.DVE, self.POOL, self.SP]
        self.bsem = self.mksem()
        self.bcnt = 0
        self.dtot = {}
        self.dbar = {}
        self.allres = []
        self.free_dsems = []

    def mksem(self):
        self.nsem += 1
        return self.top.enter_context(self.nc.semaphore(f"sem{self.nsem}"))

    def dsem(self, bar=True):
        if bar and self.free_dsems:
            return self.free_dsems.pop()
        s = self.mksem()
        self.dtot[s] = 0
        self.dbar[s] = bar
        return s

    def res(self, dma=False, bar=True):
        r = Res(self.dsem(bar) if dma else None)
        self.allres.append(r)
        return r

    def sres(self, st, dma=True):
        r = self.res(dma)
        st.callback(self.release, [r])
        return r

    def release(self, rs):
        for r in rs:
            if r.dsem is not None and self.dbar[r.dsem]:
                self.free_dsems.append(r.dsem)
            r.dsem = None

    def _need(self, E, ev, waits):
        if ev is None:
            return
        sem, val = ev
        if sem in self.dtot:
            val = self.dtot[sem]
        elif E is self.PE and sem is self.PE.sem:
            return
        if E.seen.get(sem, 0) >= val:
            return
        if waits.get(sem, 0) < val:
            waits[sem] = val

    def _sync(self, E, reads, writes):
        waits = {}
        for r in reads:
            self._need(E, r.w, waits)
        for w in writes:
            self._need(E, w.w, waits)
            for s, v in w.r.items():
                self._need(E, (s, v), waits)
        for sem, val in waits.items():
            E.eng.wait_ge(sem, val)
            E.seen[sem] = val

    def _commit(self, ev, reads, writes):
        s, v = ev
        for r in reads:
            if r.r.get(s, 0) < v:
                r.r[s] = v
        for w in writes:
            w.w = ev
            w.r = {}

    def op(self, E, fn, reads=(), writes=()):
        self._sync(E, reads, writes)
        ins = fn()
        E.cnt += 1
        ins.then_inc(E.sem, 1)
        self._commit((E.sem, E.cnt), reads, writes)

    def mmg(self, fns, reads=(), writes=()):
        E = self.PE
        self._sync(E, reads, writes)
        ins = None
        for f in fns:
            ins = f()
        E.cnt += 1
        ins.then_inc(E.sem, 1)
        self._commit((E.sem, E.cnt), reads, writes)

    def dma(self, E, out, in_, owner, reads=(), writes=()):
        self._sync(E, reads, writes)
        ins = E.eng.dma_start(out=out, in_=in_)
        sem = owner.dsem
        self.dtot[sem] += 16
        ins.then_inc(sem, 16)
        self._commit((sem, self.dtot[sem]), reads, writes)

    def barrier(self):
        sp = self.SP
        for sem, tot in self.dtot.items():
            if self.dbar[sem] and sp.seen.get(sem, 0) < tot:
                sp.eng.wait_ge(sem, tot)
                sp.seen[sem] = tot
        for E in (self.PE, self.ACT, self.DVE, self.POOL):
            if sp.seen.get(E.sem, 0) < E.cnt:
                sp.eng.wait_ge(E.sem, E.cnt)
        self.bcnt += 1
        sp.eng.sem_inc(self.bsem, 1)
        for E in (self.PE, self.ACT, self.DVE, self.POOL):
            E.eng.wait_ge(self.bsem, self.bcnt)
        for E in self.engs:
            for sem, tot in self.dtot.items():
                if self.dbar[sem]:
                    E.seen[sem] = tot
            for E2 in (self.PE, self.ACT, self.DVE, self.POOL):
                E.seen[E2.sem] = E2.cnt
        for r in self.allres:
            if r.dsem is None or self.dbar[r.dsem]:
                r.w = None
                r.r = {}
        self.allres = [r for r in self.allres if r.dsem is not None and not self.dbar[r.dsem]]


class Ring:
    def __init__(self, kb, st, name, shape, dtype, n, dma=False):
        self.kb = kb
        self.t = [st.enter_context(kb.nc.sbuf_tensor(un(f"{name}{i}"), shape, dtype)) for i in range(n)]
        self.r = [kb.res(dma) for _ in range(n)]
        self.i = 0
        st.callback(kb.release, self.r)

    def next(self):
        i = self.i
        self.i = (i + 1) % len(self.t)
        return self.t[i], self.r[i]


def build_program(cfg, dbg=None):
    c = cfg
    D, S, FF, NL, DC, FC, TP = c.D, c.S, c.FF, c.NL, c.DC, c.FC, c.TP
    NT = TP // 512
    nc = bass.Bass("TRN2", target_bir_lowering=False)
    kb = KB(nc)
    PE, ACT, DVE, POOL, SP = kb.PE, kb.ACT, kb.DVE, kb.POOL, kb.SP
    pe, act, dve, pool = nc.tensor, nc.scalar, nc.vector, nc.gpsimd

    x_in = nc.dram_tensor("x", [S, D], F32, kind="ExternalInput").ap()
    out_d = nc.dram_tensor("out", [S, D], F32, kind="ExternalOutput").ap()
    vecs_d = nc.dram_tensor("vecs", [128, c.NV], F32, kind="ExternalInput").ap()
    consts_d = nc.dram_tensor("consts", [128, c.NCC], F32, kind="ExternalInput").ap()
    rope_d = nc.dram_tensor("rope", [128, 4 * S], F32, kind="ExternalInput").ap()
    wspecs = c.wspecs()
    w_in = {(n, l): nc.dram_tensor(f"{n}_{l}", [K, N], F32, kind="ExternalInput").ap() for n, K, N in wspecs for l in range(NL)}
    wdim = {n: (K, N) for n, K, N in wspecs}
    wt = [{n: nc.dram_tensor(f"wt{par}_{n}", [N, K], BF16).ap() for n, K, N in wspecs} for par in range(2)]
    wres = [{n: kb.res(dma=True, bar=False) for n, K, N in wspecs} for par in range(2)]
    kd = dict(kind="ExternalOutput") if dbg else {}
    xT = nc.dram_tensor("xT", [D, S], F32, **kd).ap()
    qkT = nc.dram_tensor("qkT", [c.NQK * 128, S], BF16, **kd).ap()
    v_d = nc.dram_tensor("v_d", [S, c.VW], BF16, **kd).ap()
    gT = nc.dram_tensor("gT", [3 * D, S], BF16, **kd).ap()
    oT = nc.dram_tensor("oT", [c.NOC * 128, S], BF16, **kd).ap()

    conv_q = []

    def queue_conv(l):
        par = l % 2
        for n, K, N in wspecs:
            KC = K // 128
            src = w_in[n, l].rearrange("(k p) n -> p k n", p=128)
            for m in range(N // 128):
                def f(n=n, m=m, KC=KC, src=src, par=par):
                    dst = wt[par][n][m * 128:(m + 1) * 128, :].rearrange("p (k c) -> p k c", c=128)
                    kb.dma(POOL, dst, src[:, :, m * 128:(m + 1) * 128], wres[par][n], writes=[wres[par][n]])
                conv_q.append(f)

    def conv_pump(k):
        for _ in range(min(k, len(conv_q))):
            conv_q.pop(0)()

    with ExitStack() as gst:
        vecs = gst.enter_context(nc.sbuf_tensor(un("vecs_sb"), [128, c.NV], F32))
        ones_bf = gst.enter_context(nc.sbuf_tensor(un("ones_bf"), [128, 128], BF16))
        neglam = gst.enter_context(nc.sbuf_tensor(un("neglam"), [128, NL], F32))
        esink = gst.enter_context(nc.sbuf_tensor(un("esink"), [128, NL * c.AQ], F32))
        dnsc = gst.enter_context(nc.sbuf_tensor(un("dnsc"), [128, NL], F32))
        ps_t = [gst.enter_context(nc.psum_tensor(f"ps{i}", [128, 512], F32)) for i in range(8)]
        ps_r = [kb.res() for _ in range(8)]
        g_res = kb.res(dma=True)

        def lam_init(l):
            return 0.8 - 0.6 * math.exp(-0.3 * l)

        with ExitStack() as st:
            tmpa = st.enter_context(nc.sbuf_tensor(un("su_a"), [128, 64], F32))
            tmpb = st.enter_context(nc.sbuf_tensor(un("su_b"), [128, 2 * NL], F32))
            kb.dma(SP, vecs[:], vecs_d, g_res, writes=[g_res])
            kb.dma(POOL, ones_bf[:], consts_d[:, c.k_ones:c.k_ones + 128], g_res, writes=[g_res])
            kb.barrier()
            for l in range(NL):
                for j, (a, b) in enumerate(((c.v_lq1, c.v_lk1), (c.v_lq2, c.v_lk2))):
                    kb.op(DVE, lambda a=a, b=b, l=l: dve.tensor_tensor(
                        out=tmpa[:], in0=vecs[:, a + l * 64:a + (l + 1) * 64],
                        in1=vecs[:, b + l * 64:b + (l + 1) * 64], op=ALU.mult), writes=[g_res])
                    kb.op(DVE, lambda l=l, j=j: dve.reduce_sum(
                        out=tmpb[:, 2 * l + j:2 * l + j + 1], in_=tmpa[:], axis=mybir.AxisListType.X),
                        reads=[g_res], writes=[g_res])
            kb.op(ACT, lambda: act.activation(out=tmpb[:], in_=tmpb[:], func=AF.Exp), reads=[g_res], writes=[g_res])
            kb.op(ACT, lambda: act.activation(out=esink[:], in_=vecs[:, c.v_sink:c.v_sink + NL * c.AQ], func=AF.Exp),
                  reads=[g_res], writes=[g_res])
            for l in range(NL):
                kb.op(DVE, lambda l=l: dve.tensor_tensor(out=neglam[:, l:l + 1], in0=tmpb[:, 2 * l + 1:2 * l + 2],
                                                         in1=tmpb[:, 2 * l:2 * l + 1], op=ALU.subtract),
                      reads=[g_res], writes=[g_res])
                kb.op(DVE, lambda l=l: dve.tensor_scalar(out=neglam[:, l:l + 1], in0=neglam[:, l:l + 1],
                                                         scalar1=-lam_init(l), scalar2=None, op0=ALU.add),
                      reads=[g_res], writes=[g_res])
                kb.op(DVE, lambda l=l: dve.tensor_scalar(out=dnsc[:, l:l + 1], in0=vecs[:, c.v_dn + l:c.v_dn + l + 1],
                                                         scalar1=1.0 - lam_init(l), scalar2=None, op0=ALU.mult),
                      reads=[g_res], writes=[g_res])
            kb.barrier()

        queue_conv(0)
        conv_pump(len(conv_q))

        with ExitStack() as st:
            ident = st.enter_context(nc.sbuf_tensor(un("ident"), [128, 128], F32))
            kb.dma(SP, ident[:], consts_d[:, c.k_ident:c.k_ident + 128], g_res, writes=[g_res])
            xin = Ring(kb, st, "xin", [128, 4, D], F32, 2, dma=True)
            xo = Ring(kb, st, "xo", [128, 512], F32, 4, dma=True)
            pi = 0
            for j in range(S // 512):
                xt_, xr = xin.next()
                kb.dma(SP, xt_[:], x_in[j * 512:(j + 1) * 512, :].rearrange("(a p) d -> p a d", p=128), xr, writes=[xr])
                for ch in range(DC):
                    pst, psr = ps_t[pi], ps_r[pi]
                    pi = (pi + 1) % 8
                    kb.mmg([lambda a=a, ch=ch, pst=pst, xt_=xt_: pe.transpose(
                        out=pst[:, a * 128:(a + 1) * 128], in_=xt_[:, a, ch * 128:(ch + 1) * 128], identity=ident[:])
                        for a in range(4)], reads=[xr, g_res], writes=[psr])
                    ot_, orr = xo.next()
                    E, e = (DVE, dve) if ch % 2 == 0 else (ACT, act)
                    if E is DVE:
                        kb.op(DVE, lambda ot_=ot_, pst=pst: dve.tensor_copy(out=ot_[:], in_=pst[:]), reads=[psr], writes=[orr])
                    else:
                        kb.op(ACT, lambda ot_=ot_, pst=pst: act.copy(out=ot_[:], in_=pst[:]), reads=[psr], writes=[orr])
                    kb.dma(SP, xT[ch * 128:(ch + 1) * 128, j * 512:(j + 1) * 512], ot_[:], orr, reads=[orr])
            kb.barrier()

        psi = [0]

        def ps_next():
            i = psi[0]
            psi[0] = (i + 1) % 8
            return ps_t[i], ps_r[i]

        def run_units(units, PF):
            n = len(units)
            for i in range(min(PF, n)):
                units[i][0]()
            for i in range(n):
                if i + PF < n:
                    units[i + PF][0]()
                units[i][1]()
                if i >= 1:
                    units[i - 1][2]()
                conv_pump(1)
            units[n - 1][2]()

        def load_w(par, name, m, wtile, wr, k0=0, dstk0=0, KC=None):
            K, N = wdim[name]
            KC = K // 128 if KC is None else KC
            src = wt[par][name][m * 128:(m + 1) * 128, k0 * 128:(k0 + KC) * 128].rearrange("p (k c) -> p k c", c=128)
            kb.dma(SP, wtile[:, dstk0:dstk0 + KC, :], src, wr, reads=[wres[par][name]], writes=[wr])

        def norm_pass(st, t0, TPn, goff, hT, hres, eps, out_f32=False):
            NTn = TPn // 512
            xs = Ring(kb, st, "nx", [128, TPn], F32, 3, dma=True)
            sq = Ring(kb, st, "nsq", [128, TPn], BF16, 2)
            rstd = st.enter_context(nc.sbuf_tensor(un("nrstd"), [128, TPn], F32))
            rres = kb.res()
            pss = [ps_next() for _ in range(NTn)]
            for ch in range(DC):
                xt_, xr = xs.next()
                kb.dma(SP, xt_[:], xT[ch * 128:(ch + 1) * 128, t0:t0 + TPn], xr, writes=[xr])
                sq_, sr = sq.next()
                if ch % 2 == 0:
                    kb.op(DVE, lambda: dve.tensor_tensor(out=sq_[:], in0=xt_[:], in1=xt_[:], op=ALU.mult), reads=[xr], writes=[sr])
                else:
                    kb.op(POOL, lambda: pool.tensor_tensor(out=sq_[:], in0=xt_[:], in1=xt_[:], op=ALU.mult), reads=[xr], writes=[sr])
                kb.mmg([lambda t=t, sq_=sq_: pe.matmul(pss[t][0][:], lhsT=ones_bf[:], rhs=sq_[:, t * 512:(t + 1) * 512],
                                                       start=(ch == 0), stop=(ch == DC - 1)) for t in range(NTn)],
                       reads=[sr], writes=[p[1] for p in pss])
            for t in range(NTn):
                kb.op(ACT, lambda t=t: act.activation(out=rstd[:, t * 512:(t + 1) * 512], in_=pss[t][0][:], func=AF.Sqrt,
                                                      bias=float(eps), scale=1.0 / D), reads=[pss[t][1]], writes=[rres])
            kb.op(DVE, lambda: dve.reciprocal(out=rstd[:], in_=rstd[:]), reads=[rres], writes=[rres])
            for ch in range(DC):
                xt_, xr = xs.next()
                kb.dma(SP, xt_[:], xT[ch * 128:(ch + 1) * 128, t0:t0 + TPn], xr, writes=[xr])
                E, e = (DVE, dve)
                kb.op(E, lambda e=e, xt_=xt_, ch=ch: e.scalar_tensor_tensor(
                    out=hT[:, ch, 0:TPn], in0=xt_[:], scalar=vecs[:, goff + ch:goff + ch + 1], in1=rstd[:],
                    op0=ALU.mult, op1=ALU.mult), reads=[xr, rres], writes=[hres])

        def gemm_out_residual(st, par, wname, KC, aT, ares, t0, alpha, wring):
            xs = Ring(kb, st, "rx", [128, TP], F32, 4, dma=True)
            units = []
            for m in range(DC):
                stt = {}

                def load(m=m, stt=stt):
                    stt["w"] = wring.next()
                    load_w(par, wname, m, stt["w"][0], stt["w"][1])
                    stt["x"] = xs.next()
                    kb.dma(SP, stt["x"][0][:], xT[m * 128:(m + 1) * 128, t0:t0 + TP], stt["x"][1], writes=[stt["x"][1]])

                def comp(m=m, stt=stt):
                    wtile, wr = stt["w"]
                    stt["ps"] = [ps_next() for _ in range(NT)]
                    fns = []
                    for k in range(KC):
                        for t in range(NT):
                            fns.append(lambda k=k, t=t: pe.matmul(stt["ps"][t][0][:], lhsT=wtile[:, k, :],
                                                                  rhs=aT[:, k, t * 512:(t + 1) * 512],
                                                                  start=(k == 0), stop=(k == KC - 1)))
                    kb.mmg(fns, reads=[wr, ares], writes=[p[1] for p in stt["ps"]])

                def epi(m=m, stt=stt):
                    xt_, xr = stt["x"]
                    for t in range(NT):
                        kb.op(DVE, lambda t=t: dve.scalar_tensor_tensor(
                            out=xt_[:, t * 512:(t + 1) * 512], in0=stt["ps"][t][0][:], scalar=float(alpha),
                            in1=xt_[:, t * 512:(t + 1) * 512], op0=ALU.mult, op1=ALU.add),
                            reads=[stt["ps"][t][1], xr], writes=[xr])
                    kb.dma(SP, xT[m * 128:(m + 1) * 128, t0:t0 + TP], xt_[:], xr, reads=[xr])

                units.append((load, comp, epi))
            run_units(units, 2)

        def ffn_phase(l, goff, w_in_name, w_out_name):
            par = l % 2
            for p in range(S // TP):
                t0 = p * TP
                with ExitStack() as st:
                    hT = st.enter_context(nc.sbuf_tensor(un("hT"), [128, DC, TP], BF16))
                    hres = kb.res()
                    aT = st.enter_context(nc.sbuf_tensor(un("aT"), [128, FC, TP], BF16))
                    ares = kb.res()
                    wring = Ring(kb, st, "wr", [128, 32, 128], BF16, 4, dma=True)
                    with ExitStack() as st2:
                        norm_pass(st2, t0, TP, goff, hT, hres, EPS)
                        kb.barrier()
                    with ExitStack() as st2:
                        tmp = Ring(kb, st2, "ft", [128, 512], F32, 3)
                        units = []
                        for m in range(FC):
                            stt = {}

                            def load(m=m, stt=stt):
                                stt["wg"] = wring.next()
                                load_w(par, w_in_name, m, *stt["wg"])
                                stt["wu"] = wring.next()
                                load_w(par, w_in_name, FC + m, *stt["wu"])

                            def comp(m=m, stt=stt):
                                stt["pg"] = [ps_next() for _ in range(NT)]
                                stt["pu"] = [ps_next() for _ in range(NT)]
                                fns = []
                                for key, pk in (("wg", "pg"), ("wu", "pu")):
                                    wtile = stt[key][0]
                                    for k in range(DC):
                                        for t in range(NT):
                                            fns.append(lambda k=k, t=t, wtile=wtile, pk=pk: pe.matmul(
                                                stt[pk][t][0][:], lhsT=wtile[:, k, :], rhs=hT[:, k, t * 512:(t + 1) * 512],
                                                start=(k == 0), stop=(k == DC - 1)))
                                kb.mmg(fns, reads=[stt["wg"][1], stt["wu"][1], hres],
                                       writes=[p_[1] for p_ in stt["pg"] + stt["pu"]])

                            def epi(m=m, stt=stt):
                                for t in range(NT):
                                    tt, tr = tmp.next()
                                    kb.op(ACT, lambda t=t, tt=tt: act.activation(out=tt[:], in_=stt["pg"][t][0][:], func=AF.Silu),
                                          reads=[stt["pg"][t][1]], writes=[tr])
                                    kb.op(DVE, lambda t=t, tt=tt: dve.tensor_tensor(
                                        out=aT[:, m, t * 512:(t + 1) * 512], in0=stt["pu"][t][0][:], in1=tt[:], op=ALU.mult),
                                        reads=[stt["pu"][t][1], tr], writes=[ares])

                            units.append((load, comp, epi))
                        run_units(units, 1)
                        kb.barrier()
                    with ExitStack() as st2:
                        gemm_out_residual(st2, par, w_out_name, FC, aT, ares, t0, 0.5, wring)
                        kb.barrier()
                kb.barrier()

        def mixer_in_phase(l):
            par = l % 2
            goff = c.v_nm + l * DC
            for p in range(S // TP):
                t0 = p * TP
                with ExitStack() as st:
                    hT = st.enter_context(nc.sbuf_tensor(un("hT"), [128, DC, TP], BF16))
                    hres = kb.res()
                    wring = Ring(kb, st, "wr", [128, 32, 128], BF16, 4, dma=True)
                    with ExitStack() as st2:
                        norm_pass(st2, t0, TP, goff, hT, hres, EPS)
                        kb.barrier()
                    rp = st.enter_context(nc.sbuf_tensor(un("ropet"), [128, 4, TP], F32))
                    perm = st.enter_context(nc.sbuf_tensor(un("perm"), [128, 2, 128], BF16))
                    cres = kb.sres(st)
                    for i in range(4):
                        kb.dma(SP, rp[:, i, :], rope_d[:, i * S + t0:i * S + t0 + TP], cres, writes=[cres])
                    kb.dma(POOL, perm[:, 0, :], consts_d[:, c.k_permA:c.k_permA + 128], cres, writes=[cres])
                    kb.dma(POOL, perm[:, 1, :], consts_d[:, c.k_permC:c.k_permC + 128], cres, writes=[cres])
                    stage = Ring(kb, st, "stg", [128, 512], BF16, 6, dma=True)
                    qraw = Ring(kb, st, "qraw", [128, 512], BF16, 3)
                    t1r = Ring(kb, st, "t1r", [128, 512], F32, 3)
                    t2r = Ring(kb, st, "t2r", [128, 512], F32, 3)
                    units = []

                    fm = []
                    for h in range(c.AQ):
                        fm.append(("w_qkv", c.c_qa + h, 0, (qkT, c.r_qa + h)))
                    for h in range(c.AKV):
                        fm.append(("w_qkv", c.c_ka + h, 0, (qkT, c.r_ka + h)))
                    for h in range(c.BH):
                        fm.append(("w_qkv", c.c_qb + h, None, (qkT, c.r_qb + h)))
                    for h in range(c.BH):
                        fm.append(("w_qkv", c.c_kb + h, None, (qkT, c.r_kb + h)))
                    for h in range(c.CH):
                        fm.append(("w_qkv", c.c_qc + h, 1, (qkT, c.r_qc + h)))
                    for h in range(c.CH):
                        fm.append(("w_qkv", c.c_kc + h, 1, (qkT, c.r_kc + h)))
                    for m in range(3 * DC):
                        fm.append(("w_gate", m, "sig", (gT, m)))

                    for (wname, m, kind, (dst, drow)) in fm:
                        stt = {}

                        def load(wname=wname, m=m, stt=stt):
                            stt["w"] = wring.next()
                            load_w(par, wname, m, *stt["w"])

                        def comp(stt=stt):
                            wtile, wr = stt["w"]
                            stt["ps"] = [ps_next() for _ in range(NT)]
                            fns = []
                            for k in range(DC):
                                for t in range(NT):
                                    fns.append(lambda k=k, t=t: pe.matmul(stt["ps"][t][0][:], lhsT=wtile[:, k, :],
                                                                          rhs=hT[:, k, t * 512:(t + 1) * 512],
                                                                          start=(k == 0), stop=(k == DC - 1)))
                            kb.mmg(fns, reads=[wr, hres], writes=[p_[1] for p_ in stt["ps"]])

                        def epi(kind=kind, dst=dst, drow=drow, stt=stt):
                            for t in range(NT):
                                pst, psr = stt["ps"][t]
                                sg, sgr = stage.next()
                                if kind == "sig":
                                    kb.op(ACT, lambda: act.activation(out=sg[:], in_=pst[:], func=AF.Sigmoid), reads=[psr], writes=[sgr])
                                elif kind is None:
                                    kb.op(ACT, lambda: act.copy(out=sg[:], in_=pst[:]), reads=[psr], writes=[sgr])
                                else:
                                    qr_, qrr = qraw.next()
                                    kb.op(ACT, lambda: act.copy(out=qr_[:], in_=pst[:]), reads=[psr], writes=[qrr])
                                    ps2, ps2r = ps_next()
                                    kb.mmg([lambda: pe.matmul(ps2[:], lhsT=perm[:, kind, :], rhs=qr_[:], start=True, stop=True)],
                                           reads=[qrr, cres], writes=[ps2r])
                                    a1, a1r = t1r.next()
                                    a2, a2r = t2r.next()
                                    kb.op(POOL, lambda: pool.tensor_tensor(out=a1[:], in0=qr_[:], in1=rp[:, 2 * kind, t * 512:(t + 1) * 512],
                                                                           op=ALU.mult), reads=[qrr, cres], writes=[a1r])
                                    kb.op(DVE, lambda: dve.tensor_tensor(out=a2[:], in0=ps2[:], in1=rp[:, 2 * kind + 1, t * 512:(t + 1) * 512],
                                                                         op=ALU.mult), reads=[ps2r, cres], writes=[a2r])
                                    kb.op(DVE, lambda: dve.tensor_tensor(out=sg[:], in0=a1[:], in1=a2[:], op=ALU.add),
                                          reads=[a1r, a2r], writes=[sgr])
                                kb.dma(SP, dst[drow * 128:(drow + 1) * 128, t0 + t * 512:t0 + (t + 1) * 512], sg[:], sgr, reads=[sgr])

                        units.append((load, comp, epi))


                    vsrc = [(c.c_va + i, i) for i in range(c.AKV)] + [(c.c_vb + i, c.AKV + i) for i in range(c.BH)] + \
                           [(c.c_vc + i, c.AKV + c.BH + i) for i in range(c.CH)]
                    for g0 in range(0, len(vsrc), 2):
                        grp = vsrc[g0:g0 + 2]
                        stt = {}
                        for half in range(TP // 512):

                            def load(grp=grp, stt=stt, half=half):
                                if half == 0:
                                    stt["w"] = [wring.next() for _ in grp]
                                    for (m, _), w_ in zip(grp, stt["w"]):
                                        load_w(par, "w_qkv", m, *w_)

                            def comp(grp=grp, stt=stt, half=half):
                                ws = stt["w"]
                                pss = [ps_next() for _ in range(4)]
                                stt["ps", half] = pss
                                fns = []
                                for ci in range(len(grp)):
                                    for k in range(DC):
                                        for jt in range(4):
                                            tok = half * 512 + jt * 128
                                            fns.append(lambda ci=ci, k=k, jt=jt, tok=tok: pe.matmul(
                                                pss[jt][0][:, ci * 128:(ci + 1) * 128], lhsT=hT[:, k, tok:tok + 128],
                                                rhs=ws[ci][0][:, k, :], start=(k == 0), stop=(k == DC - 1)))
                                kb.mmg(fns, reads=[w_[1] for w_ in ws] + [hres], writes=[p_[1] for p_ in pss])

                            def epi(grp=grp, stt=stt, half=half):
                                ncol = len(grp) * 128
                                vcol = grp[0][1] * 128
                                for jt in range(4):
                                    pst, psr = stt["ps", half][jt]
                                    sg, sgr = stage.next()
                                    if jt % 2 == 0:
                                        kb.op(ACT, lambda: act.copy(out=sg[:, 0:ncol], in_=pst[:, 0:ncol]), reads=[psr], writes=[sgr])
                                    else:
                                        kb.op(DVE, lambda: dve.tensor_copy(out=sg[:, 0:ncol], in_=pst[:, 0:ncol]), reads=[psr], writes=[sgr])
                                    tok = t0 + half * 512 + jt * 128
                                    kb.dma(SP, v_d[tok:tok + 128, vcol:vcol + ncol], sg[:, 0:ncol], sgr, reads=[sgr])

                            units.append((load, comp, epi))
                    run_units(units, 1)
                    kb.barrier()

        SC_A = 1.0 / math.sqrt(128.0)
        SC_C = 1.0 / math.sqrt(64.0)
        NKB = S // 128
        NQT = S // 512

        def load_head(kt, kr, row, vt, vr, vcol):
            kb.dma(SP, kt[:], qkT[row * 128:(row + 1) * 128, :], kr, writes=[kr])
            kb.dma(SP, vt[:], v_d[:, vcol:vcol + 128].rearrange("(b p) d -> p b d", p=128), vr, writes=[vr])

        def attn_a(l):
            G = c.G
            with ExitStack() as st:
                masks = st.enter_context(nc.sbuf_tensor(un("mA"), [128, 2, G * 128], BF16))
                mres = kb.sres(st)
                kb.dma(POOL, masks[:, 0, :], consts_d[:, c.k_mAd:c.k_mAd + G * 128], mres, writes=[mres])
                kb.dma(POOL, masks[:, 1, :], consts_d[:, c.k_mAp:c.k_mAp + G * 128], mres, writes=[mres])
                kt = st.enter_context(nc.sbuf_tensor(un("a_kt"), [128, S], BF16)); kr = kb.sres(st)
                vt = st.enter_context(nc.sbuf_tensor(un("a_vt"), [128, NKB, 128], BF16)); vr = kb.sres(st)
                qt = st.enter_context(nc.sbuf_tensor(un("a_qt"), [128, G, S], BF16)); qr = kb.sres(st)
                oacc = st.enter_context(nc.sbuf_tensor(un("a_o"), [128, G, S], BF16)); orr = kb.sres(st)
                pe_r = Ring(kb, st, "a_pe", [128, G * 128], BF16, 4)
                pm_r = Ring(kb, st, "a_pm", [128, G * 128], BF16, 4)
                rec_r = Ring(kb, st, "a_rec", [128, G * 128], F32, 2)
                for hk in range(c.AKV):
                    load_head(kt, kr, c.r_ka + hk, vt, vr, hk * 128)
                    for g in range(G):
                        row = c.r_qa + hk * G + g
                        kb.dma(SP, qt[:, g, :], qkT[row * 128:(row + 1) * 128, :], qr, writes=[qr])
                    for qb in range(NKB):
                        kbs = [k_ for k_ in (qb - 1, qb) if k_ >= 0]
                        pms = []
                        for k_ in kbs:
                            pss, psr = ps_next()
                            kb.mmg([lambda k_=k_, pss=pss: pe.matmul(pss[:, 0:G * 128].rearrange("p (g q) -> p g q", g=G), lhsT=kt[:, k_ * 128:(k_ + 1) * 128],
                                                                     rhs=qt[:, :, qb * 128:(qb + 1) * 128], start=True, stop=True)],
                                   reads=[kr, qr], writes=[psr])
                            pe_, per = pe_r.next()
                            kb.op(ACT, lambda pss=pss, pe_=pe_: act.activation(out=pe_[:], in_=pss[:, 0:G * 128], func=AF.Exp, scale=SC_A),
                                  reads=[psr], writes=[per])
                            pm_, pmr = pm_r.next()
                            mi = 0 if k_ == qb else 1
                            kb.op(POOL, lambda pe_=pe_, pm_=pm_, mi=mi: pool.tensor_tensor(out=pm_[:], in0=pe_[:], in1=masks[:, mi, :], op=ALU.mult),
                                  reads=[per, mres], writes=[pmr])
                            pms.append((pm_, pmr, k_))
                        pso, psor = ps_next()
                        psd, psdr = ps_next()
                        fns = []
                        for i, (pm_, pmr, k_) in enumerate(pms):
                            fns.append(lambda pm_=pm_, k_=k_, i=i: pe.matmul(pso[:, 0:G * 128], lhsT=vt[:, k_, :], rhs=pm_[:],
                                                                            start=(i == 0), stop=(i == len(pms) - 1)))
                            fns.append(lambda pm_=pm_, i=i: pe.matmul(psd[:, 0:G * 128], lhsT=ones_bf[:], rhs=pm_[:],
                                                                      start=(i == 0), stop=(i == len(pms) - 1)))
                        kb.mmg(fns, reads=[vr] + [p_[1] for p_ in pms], writes=[psor, psdr])
                        rec, rr = rec_r.next()
                        for g in range(G):
                            h = hk * G + g
                            kb.op(DVE, lambda g=g, h=h, rec=rec: dve.tensor_scalar(
                                out=rec[:, g * 128:(g + 1) * 128], in0=psd[:, g * 128:(g + 1) * 128],
                                scalar1=esink[:, l * c.AQ + h:l * c.AQ + h + 1], scalar2=None, op0=ALU.add),
                                reads=[psdr, g_res], writes=[rr])
                        kb.op(DVE, lambda rec=rec: dve.reciprocal(out=rec[:], in_=rec[:]), reads=[rr], writes=[rr])
                        kb.op(DVE, lambda rec=rec: dve.tensor_tensor(
                            out=oacc[:, :, qb * 128:(qb + 1) * 128], in0=pso[:, 0:G * 128].rearrange("p (g q) -> p g q", g=G),
                            in1=rec[:].rearrange("p (g q) -> p g q", g=G), op=ALU.mult), reads=[psor, rr], writes=[orr])
                    for g in range(G):
                        row = hk * G + g
                        kb.dma(SP, oT[row * 128:(row + 1) * 128, :], oacc[:, g, :], orr, reads=[orr])
                kb.barrier()

        def attn_c(l):
            with ExitStack() as st:
                masks = st.enter_context(nc.sbuf_tensor(un("mC"), [128, 4, 512], BF16))
                mres = kb.sres(st)
                kb.dma(POOL, masks[:].rearrange("p a b -> p (a b)"), consts_d[:, c.k_mC:c.k_mC + 2048], mres, writes=[mres])
                kt = st.enter_context(nc.sbuf_tensor(un("c_kt"), [128, S], BF16)); kr = kb.sres(st)
                vt = st.enter_context(nc.sbuf_tensor(un("c_vt"), [128, NKB, 128], BF16)); vr = kb.sres(st)
                qt = st.enter_context(nc.sbuf_tensor(un("c_qt"), [128, S], BF16)); qr = kb.sres(st)
                oacc = st.enter_context(nc.sbuf_tensor(un("c_o"), [128, S], BF16)); orr = kb.sres(st)
                p_r = Ring(kb, st, "c_p", [128, 512], BF16, 6)
                f_r = Ring(kb, st, "c_f", [128, 512], F32, 6)
                sq_r = Ring(kb, st, "c_sq", [128, 512], BF16, 2)
                for h in range(c.CH):
                    load_head(kt, kr, c.r_kc + h, vt, vr, (c.AKV + c.BH + h) * 128)
                    kb.dma(SP, qt[:], qkT[(c.r_qc + h) * 128:(c.r_qc + h + 1) * 128, :], qr, writes=[qr])
                    for qi in range(NQT):
                        nkb = 4 * (qi + 1)
                        acc = [(ps_t[i], ps_r[i]) for i in range(4)]
                        sring = [4, 5, 6, 7]
                        si = 0
                        qs = slice(qi * 512, (qi + 1) * 512)
                        for k_ in range(nkb):
                            s0 = (ps_t[sring[si]], ps_r[sring[si]]); s1 = (ps_t[sring[si + 1]], ps_r[sring[si + 1]])
                            si = (si + 2) % 4
                            ks = slice(k_ * 128, (k_ + 1) * 128)
                            kb.mmg([lambda: pe.matmul(s0[0][:], lhsT=kt[0:64, ks], rhs=qt[0:64, qs], start=True, stop=True),
                                    lambda: pe.matmul(s1[0][:], lhsT=kt[64:128, ks], rhs=qt[64:128, qs], start=True, stop=True)],
                                   reads=[kr, qr], writes=[s0[1], s1[1]])
                            ps_ = []
                            for sx in (s0, s1):
                                p_, pr = p_r.next()
                                kb.op(ACT, lambda sx=sx, p_=p_: act.activation(out=p_[:], in_=sx[0][:], func=AF.Exp, scale=SC_C),
                                      reads=[sx[1]], writes=[pr])
                                if k_ >= 4 * qi:
                                    kb.op(POOL, lambda p_=p_: pool.tensor_tensor(out=p_[:], in0=p_[:], in1=masks[:, k_ - 4 * qi, :], op=ALU.mult),
                                          reads=[pr, mres], writes=[pr])
                                ps_.append((p_, pr))
                            fl = dict(start=(k_ == 0), stop=(k_ == nkb - 1))
                            kb.mmg([lambda: pe.matmul(acc[0][0][:], lhsT=vt[:, k_, :], rhs=ps_[0][0][:], **fl),
                                    lambda: pe.matmul(acc[1][0][:], lhsT=ones_bf[:], rhs=ps_[0][0][:], **fl),
                                    lambda: pe.matmul(acc[2][0][:], lhsT=vt[:, k_, :], rhs=ps_[1][0][:], **fl),
                                    lambda: pe.matmul(acc[3][0][:], lhsT=ones_bf[:], rhs=ps_[1][0][:], **fl)],
                                   reads=[vr, ps_[0][1], ps_[1][1]], writes=[a_[1] for a_ in acc])
                        r0, r0r = f_r.next(); r1, r1r = f_r.next(); o0, o0r = f_r.next(); o1, o1r = f_r.next()
                        kb.op(DVE, lambda: dve.reciprocal(out=r0[:], in_=acc[1][0][:]), reads=[acc[1][1]], writes=[r0r])
                        kb.op(DVE, lambda: dve.reciprocal(out=r1[:], in_=acc[3][0][:]), reads=[acc[3][1]], writes=[r1r])
                        kb.op(DVE, lambda: dve.tensor_tensor(out=o0[:], in0=acc[0][0][:], in1=r0[:], op=ALU.mult), reads=[acc[0][1], r0r], writes=[o0r])
                        kb.op(DVE, lambda: dve.tensor_tensor(out=o1[:], in0=acc[2][0][:], in1=r1[:], op=ALU.mult), reads=[acc[2][1], r1r], writes=[o1r])
                        kb.op(DVE, lambda: dve.scalar_tensor_tensor(out=o0[:], in0=o1[:], scalar=neglam[:, l:l + 1], in1=o0[:],
                                                                      op0=ALU.mult, op1=ALU.add), reads=[o1r, o0r, g_res], writes=[o0r])
                        sq_, sqr = sq_r.next()
                        kb.op(POOL, lambda: pool.tensor_tensor(out=sq_[:], in0=o0[:], in1=o0[:], op=ALU.mult), reads=[o0r], writes=[sqr])
                        pn = (ps_t[4], ps_r[4])
                        kb.mmg([lambda: pe.matmul(pn[0][:], lhsT=ones_bf[:], rhs=sq_[:], start=True, stop=True)], reads=[sqr], writes=[pn[1]])
                        kb.op(ACT, lambda: act.activation(out=r0[:], in_=pn[0][:], func=AF.Sqrt, bias=float(DIFF_EPS), scale=1.0 / 128.0),
                              reads=[pn[1]], writes=[r0r])
                        kb.op(DVE, lambda: dve.reciprocal(out=r0[:], in_=r0[:]), reads=[r0r], writes=[r0r])
                        kb.op(DVE, lambda: dve.scalar_tensor_tensor(out=oacc[:, qs], in0=o0[:], scalar=dnsc[:, l:l + 1], in1=r0[:],
                                                                    op0=ALU.mult, op1=ALU.mult), reads=[o0r, r0r, g_res], writes=[orr])
                    row = c.AQ + c.BH + h
                    kb.dma(SP, oT[row * 128:(row + 1) * 128, :], oacc[:], orr, reads=[orr])
                kb.barrier()

        def attn_b(l):
            with ExitStack() as st:
                masks = st.enter_context(nc.sbuf_tensor(un("mB"), [128, 4, 512], BF16))
                tri = st.enter_context(nc.sbuf_tensor(un("tri"), [128, 128], BF16))
                mres = kb.sres(st)
                kb.dma(POOL, masks[:].rearrange("p a b -> p (a b)"), consts_d[:, c.k_mB:c.k_mB + 2048], mres, writes=[mres])
                kb.dma(POOL, tri[:], consts_d[:, c.k_tri:c.k_tri + 128], mres, writes=[mres])
                kt = st.enter_context(nc.sbuf_tensor(un("b_kt"), [128, S], BF16)); kr = kb.sres(st)
                vt = st.enter_context(nc.sbuf_tensor(un("b_vt"), [128, NKB, 128], BF16)); vr = kb.sres(st)
                qt = st.enter_context(nc.sbuf_tensor(un("b_qt"), [128, S], BF16)); qr = kb.sres(st)
                oacc = st.enter_context(nc.sbuf_tensor(un("b_o"), [128, S], BF16)); orr = kb.sres(st)
                carry = st.enter_context(nc.sbuf_tensor(un("b_carry"), [128, 512], F32)); cr = kb.res()
                e_r = Ring(kb, st, "b_e", [128, 512], F32, 2)
                sp_r = Ring(kb, st, "b_sp", [128, 512], F32, 3)
                hi_r = Ring(kb, st, "b_hi", [128, 512], BF16, 3)
                lo_r = Ring(kb, st, "b_lo", [128, 512], BF16, 3)
                lb_r = Ring(kb, st, "b_lb", [128, 512], F32, 3)
                t_r = Ring(kb, st, "b_t", [128, 512], F32, 3)
                a_r = Ring(kb, st, "b_a", [128, 512], BF16, 3)
                for h in range(c.BH):
                    load_head(kt, kr, c.r_kb + h, vt, vr, (c.AKV + h) * 128)
                    kb.dma(SP, qt[:], qkT[(c.r_qb + h) * 128:(c.r_qb + h + 1) * 128, :], qr, writes=[qr])
                    for qi in range(NQT):
                        nkb = 4 * (qi + 1)
                        qs = slice(qi * 512, (qi + 1) * 512)
                        pso = (ps_t[0], ps_r[0])
                        ring = [1, 2, 3, 4, 5, 6]
                        ri = 0
                        kb.op(POOL, lambda: pool.memset(carry[:], 0.0), writes=[cr])
                        for n_, k_ in enumerate(reversed(range(nkb))):
                            pz = (ps_t[ring[ri]], ps_r[ring[ri]]); pw = (ps_t[ring[ri + 1]], ps_r[ring[ri + 1]]); pc = (ps_t[ring[ri + 2]], ps_r[ring[ri + 2]])
                            ri = (ri + 3) % 6
                            ks = slice(k_ * 128, (k_ + 1) * 128)
                            diag = k_ >= 4 * qi
                            kb.mmg([lambda: pe.matmul(pz[0][:], lhsT=kt[:, ks], rhs=qt[:, qs], start=True, stop=True)],
                                   reads=[kr, qr], writes=[pz[1]])
                            e_, er = e_r.next()
                            kb.op(ACT, lambda: act.activation(out=e_[:], in_=pz[0][:], func=AF.Exp, scale=SC_A), reads=[pz[1]], writes=[er])
                            sp_, spr = sp_r.next()
                            kb.op(ACT, lambda: act.activation(out=sp_[:], in_=e_[:], func=AF.Ln, bias=1.0, scale=1.0), reads=[er], writes=[spr])
                            if diag:
                                kb.op(POOL, lambda: pool.tensor_tensor(out=sp_[:], in0=sp_[:], in1=masks[:, k_ - 4 * qi, :], op=ALU.mult),
                                      reads=[spr, mres], writes=[spr])
                            hi_, hir = hi_r.next()
                            kb.op(POOL, lambda: pool.tensor_copy(out=hi_[:], in_=sp_[:]), reads=[spr], writes=[hir])
                            lo_, lor = lo_r.next()
                            kb.op(DVE, lambda: dve.tensor_tensor(out=lo_[:], in0=sp_[:], in1=hi_[:], op=ALU.subtract), reads=[spr, hir], writes=[lor])
                            kb.mmg([lambda: pe.matmul(pw[0][:], lhsT=tri[:], rhs=hi_[:], start=True, stop=False),
                                    lambda: pe.matmul(pw[0][:], lhsT=tri[:], rhs=lo_[:], start=False, stop=True),
                                    lambda: pe.matmul(pc[0][:], lhsT=ones_bf[:], rhs=hi_[:], start=True, stop=False),
                                    lambda: pe.matmul(pc[0][:], lhsT=ones_bf[:], rhs=lo_[:], start=False, stop=True)],
                                   reads=[hir, lor, mres], writes=[pw[1], pc[1]])
                            lb_, lbr = lb_r.next()
                            kb.op(DVE, lambda: dve.scalar_tensor_tensor(out=lb_[:], in0=pz[0][:], scalar=SC_A, in1=sp_[:],
                                                                        op0=ALU.mult, op1=ALU.subtract), reads=[pz[1], spr], writes=[lbr])
                            t_, tr = t_r.next()
                            kb.op(DVE, lambda: dve.tensor_tensor(out=t_[:], in0=pw[0][:], in1=carry[:], op=ALU.add), reads=[pw[1], cr], writes=[tr])
                            kb.op(POOL, lambda: pool.tensor_tensor(out=t_[:], in0=lb_[:], in1=t_[:], op=ALU.subtract), reads=[lbr, tr], writes=[tr])
                            a_, ar = a_r.next()
                            kb.op(ACT, lambda: act.activation(out=a_[:], in_=t_[:], func=AF.Exp), reads=[tr], writes=[ar])
                            if diag:
                                kb.op(POOL, lambda: pool.tensor_tensor(out=a_[:], in0=a_[:], in1=masks[:, k_ - 4 * qi, :], op=ALU.mult),
                                      reads=[ar, mres], writes=[ar])
                            kb.op(DVE, lambda: dve.tensor_tensor(out=carry[:], in0=pc[0][:], in1=carry[:], op=ALU.add), reads=[pc[1], cr], writes=[cr])
                            kb.mmg([lambda: pe.matmul(pso[0][:], lhsT=vt[:, k_, :], rhs=a_[:], start=(n_ == 0), stop=(n_ == nkb - 1))],
                                   reads=[vr, ar], writes=[pso[1]])
                        kb.op(ACT, lambda: act.copy(out=oacc[:, qs], in_=pso[0][:]), reads=[pso[1]], writes=[orr])
                    row = c.AQ + h
                    kb.dma(SP, oT[row * 128:(row + 1) * 128, :], oacc[:], orr, reads=[orr])
                kb.barrier()

        def mixer_out_phase(l):
            par = l % 2
            NOC = c.NOC
            br = [("w_branch_a", 0, c.AQ), ("w_branch_b", c.AQ, c.BH), ("w_branch_c", c.AQ + c.BH, c.CH)]
            for p in range(S // TP):
                t0 = p * TP
                with ExitStack() as st:
                    ot = st.enter_context(nc.sbuf_tensor(un("ot"), [128, NOC, TP], BF16)); otr = kb.sres(st)
                    mT = st.enter_context(nc.sbuf_tensor(un("mT"), [128, DC, TP], BF16)); mres_ = kb.res()
                    wring = Ring(kb, st, "wr", [128, 32, 128], BF16, 4, dma=True)
                    for ch in range(NOC):
                        kb.dma(SP, ot[:, ch, :], oT[ch * 128:(ch + 1) * 128, t0:t0 + TP], otr, writes=[otr])
                    with ExitStack() as st2:
                        gring = Ring(kb, st2, "gr", [128, 3, TP], BF16, 3 if NT >= 2 else 4, dma=True)
                        tmp = Ring(kb, st2, "mt", [128, 512], F32, 6)
                        units = []
                        for m in range(DC):
                            stt = {}
                            for t in range(NT):

                                def load(m=m, t=t, stt=stt):
                                    if t == 0:
                                        stt["w"] = wring.next()
                                        for (wn, k0, kc) in br:
                                            load_w(par, wn, m, stt["w"][0], stt["w"][1], k0=0, dstk0=k0, KC=kc)
                                        stt["g"] = gring.next()
                                        for b_ in range(3):
                                            kb.dma(SP, stt["g"][0][:, b_, :], gT[b_ * D + m * 128:b_ * D + (m + 1) * 128, t0:t0 + TP],
                                                   stt["g"][1], writes=[stt["g"][1]])

                                def comp(m=m, t=t, stt=stt):
                                    wtile, wr = stt["w"]
                                    pss = [ps_next() for _ in range(3)]
                                    stt["ps", t] = pss
                                    fns = []
                                    for b_, (wn, k0, kc) in enumerate(br):
                                        for k in range(kc):
                                            fns.append(lambda b_=b_, k=k, k0=k0, kc=kc: pe.matmul(
                                                pss[b_][0][:], lhsT=wtile[:, k0 + k, :], rhs=ot[:, k0 + k, t * 512:(t + 1) * 512],
                                                start=(k == 0), stop=(k == kc - 1)))
                                    kb.mmg(fns, reads=[wr, otr], writes=[p_[1] for p_ in pss])

                                def epi(m=m, t=t, stt=stt):
                                    gt_, gr_ = stt["g"]
                                    pss = stt["ps", t]
                                    ts_ = [tmp.next() for _ in range(3)]
                                    for b_ in range(3):
                                        kb.op(DVE, lambda b_=b_: dve.tensor_tensor(out=ts_[b_][0][:], in0=pss[b_][0][:],
                                                                                   in1=gt_[:, b_, t * 512:(t + 1) * 512], op=ALU.mult),
                                              reads=[pss[b_][1], gr_], writes=[ts_[b_][1]])
                                    kb.op(POOL, lambda: pool.tensor_tensor(out=ts_[0][0][:], in0=ts_[0][0][:], in1=ts_[1][0][:], op=ALU.add),
                                          reads=[ts_[0][1], ts_[1][1]], writes=[ts_[0][1]])
                                    kb.op(POOL, lambda: pool.tensor_tensor(out=mT[:, m, t * 512:(t + 1) * 512], in0=ts_[0][0][:], in1=ts_[2][0][:], op=ALU.add),
                                          reads=[ts_[0][1], ts_[2][1]], writes=[mres_])

                                units.append((load, comp, epi))
                        run_units(units, 2)
                        kb.barrier()
                    with ExitStack() as st2:
                        gemm_out_residual(st2, par, "w_out", DC, mT, mres_, t0, 1.0, wring)
                        kb.barrier()
                kb.barrier()

        def final_phase():
            TPF = 512
            for p in range(S // TPF):
                t0 = p * TPF
                with ExitStack() as st:
                    hF = st.enter_context(nc.sbuf_tensor(un("hF"), [128, DC, TPF], F32)); hres = kb.res()
                    ident = st.enter_context(nc.sbuf_tensor(un("identf"), [128, 128], F32)); ir = kb.sres(st)
                    kb.dma(SP, ident[:], consts_d[:, c.k_ident:c.k_ident + 128], ir, writes=[ir])
                    with ExitStack() as st2:
                        norm_pass(st2, t0, TPF, c.v_nf, hF, hres, EPS)
                        kb.barrier()
                    oring = Ring(kb, st, "fo", [128, D], F32, 2, dma=True)
                    for jt in range(TPF // 128):
                        ot_, orr = oring.next()
                        for c0 in range(0, DC, 4):
                            pst, psr = ps_next()
                            nn = min(4, DC - c0)
                            kb.mmg([lambda a=a: pe.transpose(out=pst[:, a * 128:(a + 1) * 128],
                                                             in_=hF[:, c0 + a, jt * 128:(jt + 1) * 128], identity=ident[:])
                                    for a in range(nn)], reads=[hres, ir], writes=[psr])
                            if (c0 // 4) % 2 == 0:
                                kb.op(DVE, lambda: dve.tensor_copy(out=ot_[:, c0 * 128:(c0 + nn) * 128], in_=pst[:, 0:nn * 128]), reads=[psr], writes=[orr])
                            else:
                                kb.op(ACT, lambda: act.copy(out=ot_[:, c0 * 128:(c0 + nn) * 128], in_=pst[:, 0:nn * 128]), reads=[psr], writes=[orr])
                        kb.dma(SP, out_d[t0 + jt * 128:t0 + (jt + 1) * 128, :], ot_[:], orr, reads=[orr])
                kb.barrier()

        stop = dbg or "none"
        for l in range(NL):
            if l + 1 < NL and not dbg:
                queue_conv(l + 1)
            ffn_phase(l, c.v_n1 + l * DC, "ffn1_w_in", "ffn1_w_out")
            if stop == "ffn1":
                break
            mixer_in_phase(l)
            attn_a(l)
            attn_b(l)
            attn_c(l)
            kb.barrier()
            if stop == "attn":
                break
            mixer_out_phase(l)
            if stop == "mixer":
                break
            ffn_phase(l, c.v_n2 + l * DC, "ffn2_w_in", "ffn2_w_out")
            if stop == "ffn2":
                break
            conv_pump(len(conv_q))
        if not dbg:
            final_phase()
        kb.barrier()
        for E in (PE, ACT, DVE, POOL):
            pass
    kb.top.close()
    return nc


def rope_tab(S, dim):
    inv = (1.0 / (np.float32(ROPE_THETA) ** (np.arange(0, dim, 2, dtype=np.float32) / np.float32(dim)))).astype(np.float32)
    ang = (np.arange(S, dtype=np.float32)[:, None] * inv[None, :]).astype(np.float32)
    return np.cos(ang).astype(np.float32), np.sin(ang).astype(np.float32)


def make_consts(c):
    S = c.S
    K = np.zeros((128, c.NCC), np.float32)
    K[:, c.k_ones:c.k_ones + 128] = 1.0
    K[:, c.k_ident:c.k_ident + 128] = np.eye(128, dtype=np.float32)
    k = np.arange(128)[:, None]
    m = np.arange(128)[None, :]
    K[:, c.k_permA:c.k_permA + 128] = (k == (m + 64) % 128)
    K[:, c.k_permC:c.k_permC + 128] = (k == 64 * (m // 64) + ((m % 64) + 32) % 64)
    K[:, c.k_tri:c.k_tri + 128] = (k > m)
    md = (k <= m).astype(np.float32)
    mp = (k > m).astype(np.float32)
    K[:, c.k_mAd:c.k_mAd + c.G * 128] = np.tile(md, (1, c.G))
    K[:, c.k_mAp:c.k_mAp + c.G * 128] = np.tile(mp, (1, c.G))
    q = np.arange(512)[None, :]
    for a in range(4):
        K[:, c.k_mC + a * 512:c.k_mC + (a + 1) * 512] = (k + a * 128 <= q)
        K[:, c.k_mB + a * 512:c.k_mB + (a + 1) * 512] = (k + a * 128 < q)
    rope = np.zeros((128, 4 * S), np.float32)
    ca, sa = rope_tab(S, 128)
    d = np.arange(128)
    rope[:, 0:S] = ca.T[d % 64, :]
    rope[:, S:2 * S] = sa.T[d % 64, :] * np.where(d < 64, -1.0, 1.0)[:, None].astype(np.float32)
    cc, sc = rope_tab(S, 64)
    rope[:, 2 * S:3 * S] = cc.T[(d % 64) % 32, :]
    rope[:, 3 * S:4 * S] = sc.T[(d % 64) % 32, :] * np.where((d % 64) < 32, -1.0, 1.0)[:, None].astype(np.float32)
    return K, rope


def make_vecs(c, inp):
    NL, DC = c.NL, c.DC
    V = np.zeros((128, c.NV), np.float32)

    def colmajor(a):
        a = np.asarray(a, np.float32).reshape(-1, DC, 128)
        return a.transpose(2, 0, 1).reshape(128, -1)

    V[:, c.v_n1:c.v_n1 + NL * DC] = colmajor(inp["ffn1_norm"])
    V[:, c.v_nm:c.v_nm + NL * DC] = colmajor(inp["mix_norm"])
    V[:, c.v_n2:c.v_n2 + NL * DC] = colmajor(inp["ffn2_norm"])
    V[:, c.v_nf:c.v_nf + DC] = colmajor(np.asarray(inp["final_norm"]).reshape(1, -1))
    V[:, c.v_sink:c.v_sink + NL * c.AQ] = np.asarray(inp["sinks"], np.float32).reshape(1, -1)
    for nm, off in (("lambda_q1", c.v_lq1), ("lambda_k1", c.v_lk1), ("lambda_q2", c.v_lq2), ("lambda_k2", c.v_lk2)):
        V[:, off:off + NL * 64] = np.asarray(inp[nm], np.float32).reshape(1, -1)
    V[:, c.v_dn:c.v_dn + NL] = np.asarray(inp["diff_norm"], np.float32).T
    return V


def run(cfg, inputs, dbg=None):
    c = cfg
    nc = build_program(c, dbg)
    K, rope = make_consts(c)
    V = make_vecs(c, inputs)
    x = np.asarray(inputs["x"], np.float32)
    wflat = {f"{n}_{l}": np.ascontiguousarray(np.asarray(inputs[n], np.float32)[l]) for n, Kd, N in c.wspecs() for l in range(c.NL)}
    B = x.shape[0]
    outs = []
    for b0 in range(0, B, c.NW):
        nw = min(c.NW, B - b0)
        in_maps = []
        for w in range(nw):
            m = {"x": np.ascontiguousarray(x[b0 + w]), "vecs": V, "consts": K, "rope": rope}
            m.update(wflat)
            in_maps.append(m)
        res = run_bass_kernel_spmd(nc, in_maps, core_ids=list(range(nw)))
        if dbg:
            return res.results
        outs += [res.results[w]["out"] for w in range(nw)]
    return np.stack(outs, axis=0).astype(np.float32)


def kernel(**inputs):
    return run(Cfg(), inputs)
```

```python
import math
from contextlib import ExitStack

import numpy as np
import concourse.bass as bass
import concourse.mybir as mybir
from concourse.bass_utils import run_bass_kernel_spmd

F32 = mybir.dt.float32
BF16 = mybir.dt.bfloat16
AF = mybir.ActivationFunctionType
ALU = mybir.AluOpType

ROPE_THETA = 10000.0
EPS = 1e-6
DIFF_EPS = 1e-5


class Cfg:
    def __init__(self, D=4096, S=4096, FF=4096, NL=4, AQ=16, AKV=4, BH=8, CH=8, TP=1024, NW=4, B=4):
        self.D, self.S, self.FF, self.NL = D, S, FF, NL
        self.AQ, self.AKV, self.BH, self.CH = AQ, AKV, BH, CH
        self.TP, self.NW, self.B = TP, NW, B
        self.DC, self.FC = D // 128, FF // 128
        self.G = AQ // AKV
        self.QKC = AQ + 2 * AKV + 3 * BH + 3 * CH
        o = 0
        self.c_qa = o; o += AQ
        self.c_ka = o; o += AKV
        self.c_va = o; o += AKV
        self.c_qb = o; o += BH
        self.c_kb = o; o += BH
        self.c_vb = o; o += BH
        self.c_qc = o; o += CH
        self.c_kc = o; o += CH
        self.c_vc = o; o += CH
        self.r_qa = 0
        self.r_ka = AQ
        self.r_qb = AQ + AKV
        self.r_kb = AQ + AKV + BH
        self.r_qc = AQ + AKV + 2 * BH
        self.r_kc = AQ + AKV + 2 * BH + CH
        self.NQK = AQ + AKV + 2 * BH + 2 * CH
        self.VW = (AKV + BH + CH) * 128
        self.NOC = AQ + BH + CH
        o = 0
        self.v_n1 = o; o += NL * self.DC
        self.v_nm = o; o += NL * self.DC
        self.v_n2 = o; o += NL * self.DC
        self.v_nf = o; o += self.DC
        self.v_sink = o; o += NL * AQ
        self.v_lq1 = o; o += NL * 64
        self.v_lk1 = o; o += NL * 64
        self.v_lq2 = o; o += NL * 64
        self.v_lk2 = o; o += NL * 64
        self.v_dn = o; o += NL
        self.NV = o
        o = 0
        self.k_ones = o; o += 128
        self.k_ident = o; o += 128
        self.k_permA = o; o += 128
        self.k_permC = o; o += 128
        self.k_tri = o; o += 128
        self.k_mAd = o; o += self.G * 128
        self.k_mAp = o; o += self.G * 128
        self.k_mC = o; o += 4 * 512
        self.k_mB = o; o += 4 * 512
        self.NCC = o

    def wspecs(self):
        D, FF = self.D, self.FF
        return [("ffn1_w_in", D, 2 * FF), ("ffn1_w_out", FF, D), ("w_qkv", D, self.QKC * 128),
                ("w_gate", D, 3 * D), ("w_branch_a", self.AQ * 128, D), ("w_branch_b", self.BH * 128, D),
                ("w_branch_c", self.CH * 128, D), ("w_out", D, D), ("ffn2_w_in", D, 2 * FF),
                ("ffn2_w_out", FF, D)]


_UID = [0]


def un(name):
    _UID[0] += 1
    return f"{name}_{_UID[0]}"


class EngW:
    def __init__(self, eng, sem):
        self.eng, self.sem, self.cnt, self.seen = eng, sem, 0, {}


class Res:
    __slots__ = ("w", "r", "dsem")

    def __init__(self, dsem=None):
        self.w = None
        self.r = {}
        self.dsem = dsem


class KB:
    def __init__(self, nc):
        self.nc = nc
        self.top = ExitStack()
        self.nsem = 0
        self.PE = EngW(nc.tensor, self.mksem())
        self.ACT = EngW(nc.scalar, self.mksem())
        self.DVE = EngW(nc.vector, self.mksem())
        self.POOL = EngW(nc.gpsimd, self.mksem())
        self.SP = EngW(nc.sync, None)
        self.engs = [self.PE, self.ACT, self.DVE, self.POOL, self.SP]
        self.bsem = self.mksem()
        self.bcnt = 0
        self.dtot = {}
        self.dbar = {}
        self.allres = []
        self.free_dsems = []

    def mksem(self):
        self.nsem += 1
        return self.top.enter_context(self.nc.semaphore(f"sem{self.nsem}"))

    def dsem(self, bar=True):
        if bar and self.free_dsems:
            return self.free_dsems.pop()
        s = self.mksem()
        self.dtot[s] = 0
        self.dbar[s] = bar
        return s

    def res(self, dma=False, bar=True):
        r = Res(self.dsem(bar) if dma else None)
        self.allres.append(r)
        return r

    def sres(self, st, dma=True):
        r = self.res(dma)
        st.callback(self.release, [r])
        return r

    def release(self, rs):
        for r in rs:
            if r.dsem is not None and self.dbar[r.dsem]:
                self.free_dsems.append(r.dsem)
            r.dsem = None

    def _need(self, E, ev, waits):
        if ev is None:
            return
        sem, val = ev
        if sem in self.dtot:
            val = self.dtot[sem]
        elif E is self.PE and sem is self.PE.sem:
            return
        if E.seen.get(sem, 0) >= val:
            return
        if waits.get(sem, 0) < val:
            waits[sem] = val

    def _sync(self, E, reads, writes):
        waits = {}
        for r in reads:
            self._need(E, r.w, waits)
        for w in writes:
            self._need(E, w.w, waits)
            for s, v in w.r.items():
                self._need(E, (s, v), waits)
        for sem, val in waits.items():
            E.eng.wait_ge(sem, val)
            E.seen[sem] = val

    def _commit(self, ev, reads, writes):
        s, v = ev
        for r in reads:
            if r.r.get(s, 0) < v:
                r.r[s] = v
        for w in writes:
            w.w = ev
            w.r = {}

    def op(self, E, fn, reads=(), writes=()):
        self._sync(E, reads, writes)
        ins = fn()
        E.cnt += 1
        ins.then_inc(E.sem, 1)
        self._commit((E.sem, E.cnt), reads, writes)

    def mmg(self, fns, reads=(), writes=()):
        E = self.PE
        self._sync(E, reads, writes)
        ins = None
        for f in fns:
            ins = f()
        E.cnt += 1
        ins.then_inc(E.sem, 1)
        self._commit((E.sem, E.cnt), reads, writes)

    def dma(self, E, out, in_, owner, reads=(), writes=()):
        self._sync(E, reads, writes)
        ins = E.eng.dma_start(out=out, in_=in_)
        sem = owner.dsem
        self.dtot[sem] += 16
        ins.then_inc(sem, 16)
        self._commit((sem, self.dtot[sem]), reads, writes)

    def barrier(self):
        sp = self.SP
        for sem, tot in self.dtot.items():
            if self.dbar[sem] and sp.seen.get(sem, 0) < tot:
                sp.eng.wait_ge(sem, tot)
                sp.seen[sem] = tot
        for E in (self.PE, self.ACT, self.DVE, self.POOL):
            if sp.seen.get(E.sem, 0) < E.cnt:
                sp.eng.wait_ge(E.sem, E.cnt)
        self.bcnt += 1
        sp.eng.sem_inc(self.bsem, 1)
        for E in (self.PE, self.ACT, self.DVE, self.POOL):
            E.eng.wait_ge(self.bsem, self.bcnt)
        for E in self.engs:
            for sem, tot in self.dtot.items():
                if self.dbar[sem]:
                    E.seen[sem] = tot
            for E2 in (self.PE, self.ACT, self.DVE, self.POOL):
                E.seen[E2.sem] = E2.cnt
        for r in self.allres:
            if r.dsem is None or self.dbar[r.dsem]:
                r.w = None
                r.r = {}
        self.allres = [r for r in self.allres if r.dsem is not None and not self.dbar[r.dsem]]


class Ring:
    def __init__(self, kb, st, name, shape, dtype, n, dma=False):
        self.kb = kb
        self.t = [st.enter_context(kb.nc.sbuf_tensor(un(f"{name}{i}"), shape, dtype)) for i in range(n)]
        self.r = [kb.res(dma) for _ in range(n)]
        self.i = 0
        st.callback(kb.release, self.r)

    def next(self):
        i = self.i
        self.i = (i + 1) % len(self.t)
        return self.t[i], self.r[i]


def build_program(cfg, dbg=None):
    c = cfg
    D, S, FF, NL, DC, FC, TP = c.D, c.S, c.FF, c.NL, c.DC, c.FC, c.TP
    NT = TP // 512
    nc = bass.Bass("TRN2", target_bir_lowering=False)
    kb = KB(nc)
    PE, ACT, DVE, POOL, SP = kb.PE, kb.ACT, kb.DVE, kb.POOL, kb.SP
    pe, act, dve, pool = nc.tensor, nc.scalar, nc.vector, nc.gpsimd

    x_in = nc.dram_tensor("x", [S, D], F32, kind="ExternalInput").ap()
    out_d = nc.dram_tensor("out", [S, D], F32, kind="ExternalOutput").ap()
    vecs_d = nc.dram_tensor("vecs", [128, c.NV], F32, kind="ExternalInput").ap()
    consts_d = nc.dram_tensor("consts", [128, c.NCC], F32, kind="ExternalInput").ap()
    rope_d = nc.dram_tensor("rope", [128, 4 * S], F32, kind="ExternalInput").ap()
    wspecs = c.wspecs()
    w_in = {(n, l): nc.dram_tensor(f"{n}_{l}", [K, N], F32, kind="ExternalInput").ap() for n, K, N in wspecs for l in range(NL)}
    wdim = {n: (K, N) for n, K, N in wspecs}
    wt = [{n: nc.dram_tensor(f"wt{par}_{n}", [N, K], BF16).ap() for n, K, N in wspecs} for par in range(2)]
    wres = [{n: kb.res(dma=True, bar=False) for n, K, N in wspecs} for par in range(2)]
    kd = dict(kind="ExternalOutput") if dbg else {}
    xT = nc.dram_tensor("xT", [D, S], F32, **kd).ap()
    qkT = nc.dram_tensor("qkT", [c.NQK * 128, S], BF16, **kd).ap()
    v_d = nc.dram_tensor("v_d", [S, c.VW], BF16, **kd).ap()
    gT = nc.dram_tensor("gT", [3 * D, S], BF16, **kd).ap()
    oT = nc.dram_tensor("oT", [c.NOC * 128, S], BF16, **kd).ap()

    conv_q = []

    def queue_conv(l):
        par = l % 2
        for n, K, N in wspecs:
            KC = K // 128
            src = w_in[n, l].rearrange("(k p) n -> p k n", p=128)
            for m in range(N // 128):
                def f(n=n, m=m, KC=KC, src=src, par=par):
                    dst = wt[par][n][m * 128:(m + 1) * 128, :].rearrange("p (k c) -> p k c", c=128)
                    kb.dma(POOL, dst, src[:, :, m * 128:(m + 1) * 128], wres[par][n], writes=[wres[par][n]])
                conv_q.append(f)

    def conv_pump(k):
        for _ in range(min(k, len(conv_q))):
            conv_q.pop(0)()

    with ExitStack() as gst:
        vecs = gst.enter_context(nc.sbuf_tensor(un("vecs_sb"), [128, c.NV], F32))
        ones_bf = gst.enter_context(nc.sbuf_tensor(un("ones_bf"), [128, 128], BF16))
        neglam = gst.enter_context(nc.sbuf_tensor(un("neglam"), [128, NL], F32))
        esink = gst.enter_context(nc.sbuf_tensor(un("esink"), [128, NL * c.AQ], F32))
        dnsc = gst.enter_context(nc.sbuf_tensor(un("dnsc"), [128, NL], F32))
        ps_t = [gst.enter_context(nc.psum_tensor(f"ps{i}", [128, 512], F32)) for i in range(8)]
        ps_r = [kb.res() for _ in range(8)]
        g_res = kb.res(dma=True)

        def lam_init(l):
            return 0.8 - 0.6 * math.exp(-0.3 * l)

        with ExitStack() as st:
            tmpa = st.enter_context(nc.sbuf_tensor(un("su_a"), [128, 64], F32))
            tmpb = st.enter_context(nc.sbuf_tensor(un("su_b"), [128, 2 * NL], F32))
            kb.dma(SP, vecs[:], vecs_d, g_res, writes=[g_res])
            kb.dma(POOL, ones_bf[:], consts_d[:, c.k_ones:c.k_ones + 128], g_res, writes=[g_res])
            kb.barrier()
            for l in range(NL):
                for j, (a, b) in enumerate(((c.v_lq1, c.v_lk1), (c.v_lq2, c.v_lk2))):
                    kb.op(DVE, lambda a=a, b=b, l=l: dve.tensor_tensor(
                        out=tmpa[:], in0=vecs[:, a + l * 64:a + (l + 1) * 64],
                        in1=vecs[:, b + l * 64:b + (l + 1) * 64], op=ALU.mult), writes=[g_res])
                    kb.op(DVE, lambda l=l, j=j: dve.reduce_sum(
                        out=tmpb[:, 2 * l + j:2 * l + j + 1], in_=tmpa[:], axis=mybir.AxisListType.X),
                        reads=[g_res], writes=[g_res])
            kb.op(ACT, lambda: act.activation(out=tmpb[:], in_=tmpb[:], func=AF.Exp), reads=[g_res], writes=[g_res])
            kb.op(ACT, lambda: act.activation(out=esink[:], in_=vecs[:, c.v_sink:c.v_sink + NL * c.AQ], func=AF.Exp),
                  reads=[g_res], writes=[g_res])
            for l in range(NL):
                kb.op(DVE, lambda l=l: dve.tensor_tensor(out=neglam[:, l:l + 1], in0=tmpb[:, 2 * l + 1:2 * l + 2],
                                                         in1=tmpb[:, 2 * l:2 * l + 1], op=ALU.subtract),
                      reads=[g_res], writes=[g_res])
                kb.op(DVE, lambda l=l: dve.tensor_scalar(out=neglam[:, l:l + 1], in0=neglam[:, l:l + 1],
                                                         scalar1=-lam_init(l), scalar2=None, op0=ALU.add),
                      reads=[g_res], writes=[g_res])
                kb.op(DVE, lambda l=l: dve.tensor_scalar(out=dnsc[:, l:l + 1], in0=vecs[:, c.v_dn + l:c.v_dn + l + 1],
                                                         scalar1=1.0 - lam_init(l), scalar2=None, op0=ALU.mult),
                      reads=[g_res], writes=[g_res])
            kb.barrier()

        queue_conv(0)
        conv_pump(len(conv_q))

        with ExitStack() as st:
            ident = st.enter_context(nc.sbuf_tensor(un("ident"), [128, 128], F32))
            kb.dma(SP, ident[:], consts_d[:, c.k_ident:c.k_ident + 128], g_res, writes=[g_res])
            xin = Ring(kb, st, "xin", [128, 4, D], F32, 2, dma=True)
            xo = Ring(kb, st, "xo", [128, 512], F32, 4, dma=True)
            pi = 0
            for j in range(S // 512):
                xt_, xr = xin.next()
                kb.dma(SP, xt_[:], x_in[j * 512:(j + 1) * 512, :].rearrange("(a p) d -> p a d", p=128), xr, writes=[xr])
                for ch in range(DC):
                    pst, psr = ps_t[pi], ps_r[pi]
                    pi = (pi + 1) % 8
                    kb.mmg([lambda a=a, ch=ch, pst=pst, xt_=xt_: pe.transpose(
                        out=pst[:, a * 128:(a + 1) * 128], in_=xt_[:, a, ch * 128:(ch + 1) * 128], identity=ident[:])
                        for a in range(4)], reads=[xr, g_res], writes=[psr])
                    ot_, orr = xo.next()
                    E, e = (DVE, dve) if ch % 2 == 0 else (ACT, act)
                    if E is DVE:
                        kb.op(DVE, lambda ot_=ot_, pst=pst: dve.tensor_copy(out=ot_[:], in_=pst[:]), reads=[psr], writes=[orr])
                    else:
                        kb.op(ACT, lambda ot_=ot_, pst=pst: act.copy(out=ot_[:], in_=pst[:]), reads=[psr], writes=[orr])
                    kb.dma(SP, xT[ch * 128:(ch + 1) * 128, j * 512:(j + 1) * 512], ot_[:], orr, reads=[orr])
            kb.barrier()

        psi = [0]

        def ps_next():
            i = psi[0]
            psi[0] = (i + 1) % 8
            return ps_t[i], ps_r[i]

        def run_units(units, PF):
            n = len(units)
            for i in range(min(PF, n)):
                units[i][0]()
            for i in range(n):
                if i + PF < n:
                    units[i + PF][0]()
                units[i][1]()
                if i >= 1:
                    units[i - 1][2]()
                conv_pump(1)
            units[n - 1][2]()

        def load_w(par, name, m, wtile, wr, k0=0, dstk0=0, KC=None):
            K, N = wdim[name]
            KC = K // 128 if KC is None else KC
            src = wt[par][name][m * 128:(m + 1) * 128, k0 * 128:(k0 + KC) * 128].rearrange("p (k c) -> p k c", c=128)
            kb.dma(SP, wtile[:, dstk0:dstk0 + KC, :], src, wr, reads=[wres[par][name]], writes=[wr])

        def norm_pass(st, t0, TPn, goff, hT, hres, eps, out_f32=False):
            NTn = TPn // 512
            xs = Ring(kb, st, "nx", [128, TPn], F32, 3, dma=True)
            sq = Ring(kb, st, "nsq", [128, TPn], BF16, 2)
            rstd = st.enter_context(nc.sbuf_tensor(un("nrstd"), [128, TPn], F32))
            rres = kb.res()
            pss = [ps_next() for _ in range(NTn)]
            for ch in range(DC):
                xt_, xr = xs.next()
                kb.dma(SP, xt_[:], xT[ch * 128:(ch + 1) * 128, t0:t0 + TPn], xr, writes=[xr])
                sq_, sr = sq.next()
                if ch % 2 == 0:
                    kb.op(DVE, lambda: dve.tensor_tensor(out=sq_[:], in0=xt_[:], in1=xt_[:], op=ALU.mult), reads=[xr], writes=[sr])
                else:
                    kb.op(POOL, lambda: pool.tensor_tensor(out=sq_[:], in0=xt_[:], in1=xt_[:], op=ALU.mult), reads=[xr], writes=[sr])
                kb.mmg([lambda t=t, sq_=sq_: pe.matmul(pss[t][0][:], lhsT=ones_bf[:], rhs=sq_[:, t * 512:(t + 1) * 512],
                                                       start=(ch == 0), stop=(ch == DC - 1)) for t in range(NTn)],
                       reads=[sr], writes=[p[1] for p in pss])
            for t in range(NTn):
                kb.op(ACT, lambda t=t: act.activation(out=rstd[:, t * 512:(t + 1) * 512], in_=pss[t][0][:], func=AF.Sqrt,
                                                      bias=float(eps), scale=1.0 / D), reads=[pss[t][1]], writes=[rres])
            kb.op(DVE, lambda: dve.reciprocal(out=rstd[:], in_=rstd[:]), reads=[rres], writes=[rres])
            for ch in range(DC):
                xt_, xr = xs.next()
                kb.dma(SP, xt_[:], xT[ch * 128:(ch + 1) * 128, t0:t0 + TPn], xr, writes=[xr])
                E, e = (DVE, dve)
                kb.op(E, lambda e=e, xt_=xt_, ch=ch: e.scalar_tensor_tensor(
                    out=hT[:, ch, 0:TPn], in0=xt_[:], scalar=vecs[:, goff + ch:goff + ch + 1], in1=rstd[:],
                    op0=ALU.mult, op1=ALU.mult), reads=[xr, rres], writes=[hres])

        def gemm_out_residual(st, par, wname, KC, aT, ares, t0, alpha, wring):
            xs = Ring(kb, st, "rx", [128, TP], F32, 4, dma=True)
            units = []
            for m in range(DC):
                stt = {}

                def load(m=m, stt=stt):
                    stt["w"] = wring.next()
                    load_w(par, wname, m, stt["w"][0], stt["w"][1])
                    stt["x"] = xs.next()
                    kb.dma(SP, stt["x"][0][:], xT[m * 128:(m + 1) * 128, t0:t0 + TP], stt["x"][1], writes=[stt["x"][1]])

                def comp(m=m, stt=stt):
                    wtile, wr = stt["w"]
                    stt["ps"] = [ps_next() for _ in range(NT)]
                    fns = []
                    for k in range(KC):
                        for t in range(NT):
                            fns.append(lambda k=k, t=t: pe.matmul(stt["ps"][t][0][:], lhsT=wtile[:, k, :],
                                                                  rhs=aT[:, k, t * 512:(t + 1) * 512],
                                                                  start=(k == 0), stop=(k == KC - 1)))
                    kb.mmg(fns, reads=[wr, ares], writes=[p[1] for p in stt["ps"]])

                def epi(m=m, stt=stt):
                    xt_, xr = stt["x"]
                    for t in range(NT):
                        kb.op(DVE, lambda t=t: dve.scalar_tensor_tensor(
                            out=xt_[:, t * 512:(t + 1) * 512], in0=stt["ps"][t][0][:], scalar=float(alpha),
                            in1=xt_[:, t * 512:(t + 1) * 512], op0=ALU.mult, op1=ALU.add),
                            reads=[stt["ps"][t][1], xr], writes=[xr])
                    kb.dma(SP, xT[m * 128:(m + 1) * 128, t0:t0 + TP], xt_[:], xr, reads=[xr])

                units.append((load, comp, epi))
            run_units(units, 2)

        def ffn_phase(l, goff, w_in_name, w_out_name):
            par = l % 2
            for p in range(S // TP):
                t0 = p * TP
                with ExitStack() as st:
                    hT = st.enter_context(nc.sbuf_tensor(un("hT"), [128, DC, TP], BF16))
                    hres = kb.res()
                    aT = st.enter_context(nc.sbuf_tensor(un("aT"), [128, FC, TP], BF16))
                    ares = kb.res()
                    wring = Ring(kb, st, "wr", [128, 32, 128], BF16, 4, dma=True)
                    with ExitStack() as st2:
                        norm_pass(st2, t0, TP, goff, hT, hres, EPS)
                        kb.barrier()
                    with ExitStack() as st2:
                        tmp = Ring(kb, st2, "ft", [128, 512], F32, 3)
                        units = []
                        for m in range(FC):
                            stt = {}

                            def load(m=m, stt=stt):
                                stt["wg"] = wring.next()
                                load_w(par, w_in_name, m, *stt["wg"])
                                stt["wu"] = wring.next()
                                load_w(par, w_in_name, FC + m, *stt["wu"])

                            def comp(m=m, stt=stt):
                                stt["pg"] = [ps_next() for _ in range(NT)]
                                stt["pu"] = [ps_next() for _ in range(NT)]
                                fns = []
                                for key, pk in (("wg", "pg"), ("wu", "pu")):
                                    wtile = stt[key][0]
                                    for k in range(DC):
                                        for t in range(NT):
                                            fns.append(lambda k=k, t=t, wtile=wtile, pk=pk: pe.matmul(
                                                stt[pk][t][0][:], lhsT=wtile[:, k, :], rhs=hT[:, k, t * 512:(t + 1) * 512],
                                                start=(k == 0), stop=(k == DC - 1)))
                                kb.mmg(fns, reads=[stt["wg"][1], stt["wu"][1], hres],
                                       writes=[p_[1] for p_ in stt["pg"] + stt["pu"]])

                            def epi(m=m, stt=stt):
                                for t in range(NT):
                                    tt, tr = tmp.next()
                                    kb.op(ACT, lambda t=t, tt=tt: act.activation(out=tt[:], in_=stt["pg"][t][0][:], func=AF.Silu),
                                          reads=[stt["pg"][t][1]], writes=[tr])
                                    kb.op(DVE, lambda t=t, tt=tt: dve.tensor_tensor(
                                        out=aT[:, m, t * 512:(t + 1) * 512], in0=stt["pu"][t][0][:], in1=tt[:], op=ALU.mult),
                                        reads=[stt["pu"][t][1], tr], writes=[ares])

                            units.append((load, comp, epi))
                        run_units(units, 1)
                        kb.barrier()
                    with ExitStack() as st2:
                        gemm_out_residual(st2, par, w_out_name, FC, aT, ares, t0, 0.5, wring)
                        kb.barrier()
                kb.barrier()

        def mixer_in_phase(l):
            par = l % 2
            goff = c.v_nm + l * DC
            for p in range(S // TP):
                t0 = p * TP
                with ExitStack() as st:
                    hT = st.enter_context(nc.sbuf_tensor(un("hT"), [128, DC, TP], BF16))
                    hres = kb.res()
                    wring = Ring(kb, st, "wr", [128, 32, 128], BF16, 4, dma=True)
                    with ExitStack() as st2:
                        norm_pass(st2, t0, TP, goff, hT, hres, EPS)
                        kb.barrier()
                    rp = st.enter_context(nc.sbuf_tensor(un("ropet"), [128, 4, TP], F32))
                    perm = st.enter_context(nc.sbuf_tensor(un("perm"), [128, 2, 128], BF16))
                    cres = kb.sres(st)
                    for i in range(4):
                        kb.dma(SP, rp[:, i, :], rope_d[:, i * S + t0:i * S + t0 + TP], cres, writes=[cres])
                    kb.dma(POOL, perm[:, 0, :], consts_d[:, c.k_permA:c.k_permA + 128], cres, writes=[cres])
                    kb.dma(POOL, perm[:, 1, :], consts_d[:, c.k_permC:c.k_permC + 128], cres, writes=[cres])
                    stage = Ring(kb, st, "stg", [128, 512], BF16, 6, dma=True)
                    qraw = Ring(kb, st, "qraw", [128, 512], BF16, 3)
                    t1r = Ring(kb, st, "t1r", [128, 512], F32, 3)
                    t2r = Ring(kb, st, "t2r", [128, 512], F32, 3)
                    units = []

                    fm = []
                    for h in range(c.AQ):
                        fm.append(("w_qkv", c.c_qa + h, 0, (qkT, c.r_qa + h)))
                    for h in range(c.AKV):
                        fm.append(("w_qkv", c.c_ka + h, 0, (qkT, c.r_ka + h)))
                    for h in range(c.BH):
                        fm.append(("w_qkv", c.c_qb + h, None, (qkT, c.r_qb + h)))
                    for h in range(c.BH):
                        fm.append(("w_qkv", c.c_kb + h, None, (qkT, c.r_kb + h)))
                    for h in range(c.CH):
                        fm.append(("w_qkv", c.c_qc + h, 1, (qkT, c.r_qc + h)))
                    for h in range(c.CH):
                        fm.append(("w_qkv", c.c_kc + h, 1, (qkT, c.r_kc + h)))
                    for m in range(3 * DC):
                        fm.append(("w_gate", m, "sig", (gT, m)))

                    for (wname, m, kind, (dst, drow)) in fm:
                        stt = {}

                        def load(wname=wname, m=m, stt=stt):
                            stt["w"] = wring.next()
                            load_w(par, wname, m, *stt["w"])

                        def comp(stt=stt):
                            wtile, wr = stt["w"]
                            stt["ps"] = [ps_next() for _ in range(NT)]
                            fns = []
                            for k in range(DC):
                                for t in range(NT):
                                    fns.append(lambda k=k, t=t: pe.matmul(stt["ps"][t][0][:], lhsT=wtile[:, k, :],
                                                                          rhs=hT[:, k, t * 512:(t + 1) * 512],
                                                                          start=(k == 0), stop=(k == DC - 1)))
                            kb.mmg(fns, reads=[wr, hres], writes=[p_[1] for p_ in stt["ps"]])

                        def epi(kind=kind, dst=dst, drow=drow, stt=stt):
                            for t in range(NT):
                                pst, psr = stt["ps"][t]
                                sg, sgr = stage.next()
                                if kind == "sig":
                                    kb.op(ACT, lambda: act.activation(out=sg[:], in_=pst[:], func=AF.Sigmoid), reads=[psr], writes=[sgr])
                                elif kind is None:
                                    kb.op(ACT, lambda: act.copy(out=sg[:], in_=pst[:]), reads=[psr], writes=[sgr])
                                else:
                                    qr_, qrr = qraw.next()
                                    kb.op(ACT, lambda: act.copy(out=qr_[:], in_=pst[:]), reads=[psr], writes=[qrr])
                                    ps2, ps2r = ps_next()
                                    kb.mmg([lambda: pe.matmul(ps2[:], lhsT=perm[:, kind, :], rhs=qr_[:], start=True, stop=True)],
                                           reads=[qrr, cres], writes=[ps2r])
                                    a1, a1r = t1r.next()
                                    a2, a2r = t2r.next()
                                    kb.op(POOL, lambda: pool.tensor_tensor(out=a1[:], in0=qr_[:], in1=rp[:, 2 * kind, t * 512:(t + 1) * 512],
                                                                           op=ALU.mult), reads=[qrr, cres], writes=[a1r])
                                    kb.op(DVE, lambda: dve.tensor_tensor(out=a2[:], in0=ps2[:], in1=rp[:, 2 * kind + 1, t * 512:(t + 1) * 512],
                                                                         op=ALU.mult), reads=[ps2r, cres], writes=[a2r])
                                    kb.op(DVE, lambda: dve.tensor_tensor(out=sg[:], in0=a1[:], in1=a2[:], op=ALU.add),
                                          reads=[a1r, a2r], writes=[sgr])
                                kb.dma(SP, dst[drow * 128:(drow + 1) * 128, t0 + t * 512:t0 + (t + 1) * 512], sg[:], sgr, reads=[sgr])

                        units.append((load, comp, epi))


                    vsrc = [(c.c_va + i, i) for i in range(c.AKV)] + [(c.c_vb + i, c.AKV + i) for i in range(c.BH)] + \
                           [(c.c_vc + i, c.AKV + c.BH + i) for i in range(c.CH)]
                    for g0 in range(0, len(vsrc), 2):
                        grp = vsrc[g0:g0 + 2]
                        stt = {}
                        for half in range(TP // 512):

                            def load(grp=grp, stt=stt, half=half):
                                if half == 0:
                                    stt["w"] = [wring.next() for _ in grp]
                                    for (m, _), w_ in zip(grp, stt["w"]):
                                        load_w(par, "w_qkv", m, *w_)

                            def comp(grp=grp, stt=stt, half=half):
                                ws = stt["w"]
                                pss = [ps_next() for _ in range(4)]
                                stt["ps", half] = pss
                                fns = []
                                for ci in range(len(grp)):
                                    for k in range(DC):
                                        for jt in range(4):
                                            tok = half * 512 + jt * 128
                                            fns.append(lambda ci=ci, k=k, jt=jt, tok=tok: pe.matmul(
                                                pss[jt][0][:, ci * 128:(ci + 1) * 128], lhsT=hT[:, k, tok:tok + 128],
                                                rhs=ws[ci][0][:, k, :], start=(k == 0), stop=(k == DC - 1)))
                                kb.mmg(fns, reads=[w_[1] for w_ in ws] + [hres], writes=[p_[1] for p_ in pss])

                            def epi(grp=grp, stt=stt, half=half):
                                ncol = len(grp) * 128
                                vcol = grp[0][1] * 128
                                for jt in range(4):
                                    pst, psr = stt["ps", half][jt]
                                    sg, sgr = stage.next()
                                    if jt % 2 == 0:
                                        kb.op(ACT, lambda: act.copy(out=sg[:, 0:ncol], in_=pst[:, 0:ncol]), reads=[psr], writes=[sgr])
                                    else:
                                        kb.op(DVE, lambda: dve.tensor_copy(out=sg[:, 0:ncol], in_=pst[:, 0:ncol]), reads=[psr], writes=[sgr])
                                    tok = t0 + half * 512 + jt * 128
                                    kb.dma(SP, v_d[tok:tok + 128, vcol:vcol + ncol], sg[:, 0:ncol], sgr, reads=[sgr])

                            units.append((load, comp, epi))
                    run_units(units, 1)
                    kb.barrier()

        SC_A = 1.0 / math.sqrt(128.0)
        SC_C = 1.0 / math.sqrt(64.0)
        NKB = S // 128
        NQT = S // 512

        def load_head(kt, kr, row, vt, vr, vcol):
            kb.dma(SP, kt[:], qkT[row * 128:(row + 1) * 128, :], kr, writes=[kr])
            kb.dma(SP, vt[:], v_d[:, vcol:vcol + 128].rearrange("(b p) d -> p b d", p=128), vr, writes=[vr])

        def attn_a(l):
            G = c.G
            with ExitStack() as st:
                masks = st.enter_context(nc.sbuf_tensor(un("mA"), [128, 2, G * 128], BF16))
                mres = kb.sres(st)
                kb.dma(POOL, masks[:, 0, :], consts_d[:, c.k_mAd:c.k_mAd + G * 128], mres, writes=[mres])
                kb.dma(POOL, masks[:, 1, :], consts_d[:, c.k_mAp:c.k_mAp + G * 128], mres, writes=[mres])
                kt = st.enter_context(nc.sbuf_tensor(un("a_kt"), [128, S], BF16)); kr = kb.sres(st)
                vt = st.enter_context(nc.sbuf_tensor(un("a_vt"), [128, NKB, 128], BF16)); vr = kb.sres(st)
                qt = st.enter_context(nc.sbuf_tensor(un("a_qt"), [128, G, S], BF16)); qr = kb.sres(st)
                oacc = st.enter_context(nc.sbuf_tensor(un("a_o"), [128, G, S], BF16)); orr = kb.sres(st)
                pe_r = Ring(kb, st, "a_pe", [128, G * 128], BF16, 4)
                pm_r = Ring(kb, st, "a_pm", [128, G * 128], BF16, 4)
                rec_r = Ring(kb, st, "a_rec", [128, G * 128], F32, 2)
                for hk in range(c.AKV):
                    load_head(kt, kr, c.r_ka + hk, vt, vr, hk * 128)
                    for g in range(G):
                        row = c.r_qa + hk * G + g
                        kb.dma(SP, qt[:, g, :], qkT[row * 128:(row + 1) * 128, :], qr, writes=[qr])
                    for qb in range(NKB):
                        kbs = [k_ for k_ in (qb - 1, qb) if k_ >= 0]
                        pms = []
                        for k_ in kbs:
                            pss, psr = ps_next()
                            kb.mmg([lambda k_=k_, pss=pss: pe.matmul(pss[:, 0:G * 128].rearrange("p (g q) -> p g q", g=G), lhsT=kt[:, k_ * 128:(k_ + 1) * 128],
                                                                     rhs=qt[:, :, qb * 128:(qb + 1) * 128], start=True, stop=True)],
                                   reads=[kr, qr], writes=[psr])
                            pe_, per = pe_r.next()
                            kb.op(ACT, lambda pss=pss, pe_=pe_: act.activation(out=pe_[:], in_=pss[:, 0:G * 128], func=AF.Exp, scale=SC_A),
                                  reads=[psr], writes=[per])
                            pm_, pmr = pm_r.next()
                            mi = 0 if k_ == qb else 1
                            kb.op(POOL, lambda pe_=pe_, pm_=pm_, mi=mi: pool.tensor_tensor(out=pm_[:], in0=pe_[:], in1=masks[:, mi, :], op=ALU.mult),
                                  reads=[per, mres], writes=[pmr])
                            pms.append((pm_, pmr, k_))
                        pso, psor = ps_next()
                        psd, psdr = ps_next()
                        fns = []
                        for i, (pm_, pmr, k_) in enumerate(pms):
                            fns.append(lambda pm_=pm_, k_=k_, i=i: pe.matmul(pso[:, 0:G * 128], lhsT=vt[:, k_, :], rhs=pm_[:],
                                                                            start=(i == 0), stop=(i == len(pms) - 1)))
                            fns.append(lambda pm_=pm_, i=i: pe.matmul(psd[:, 0:G * 128], lhsT=ones_bf[:], rhs=pm_[:],
                                                                      start=(i == 0), stop=(i == len(pms) - 1)))
                        kb.mmg(fns, reads=[vr] + [p_[1] for p_ in pms], writes=[psor, psdr])
                        rec, rr = rec_r.next()
                        for g in range(G):
                            h = hk * G + g
                            kb.op(DVE, lambda g=g, h=h, rec=rec: dve.tensor_scalar(
                                out=rec[:, g * 128:(g + 1) * 128], in0=psd[:, g * 128:(g + 1) * 128],
                                scalar1=esink[:, l * c.AQ + h:l * c.AQ + h + 1], scalar2=None, op0=ALU.add),
                                reads=[psdr, g_res], writes=[rr])
                        kb.op(DVE, lambda rec=rec: dve.reciprocal(out=rec[:], in_=rec[:]), reads=[rr], writes=[rr])
                        kb.op(DVE, lambda rec=rec: dve.tensor_tensor(
                            out=oacc[:, :, qb * 128:(qb + 1) * 128], in0=pso[:, 0:G * 128].rearrange("p (g q) -> p g q", g=G),
                            in1=rec[:].rearrange("p (g q) -> p g q", g=G), op=ALU.mult), reads=[psor, rr], writes=[orr])
                    for g in range(G):
                        row = hk * G + g
                        kb.dma(SP, oT[row * 128:(row + 1) * 128, :], oacc[:, g, :], orr, reads=[orr])
                kb.barrier()

        def attn_c(l):
            with ExitStack() as st:
                masks = st.enter_context(nc.sbuf_tensor(un("mC"), [128, 4, 512], BF16))
                mres = kb.sres(st)
                kb.dma(POOL, masks[:].rearrange("p a b -> p (a b)"), consts_d[:, c.k_mC:c.k_mC + 2048], mres, writes=[mres])
                kts = [st.enter_context(nc.sbuf_tensor(un("c_kt"), [128, S], BF16)) for _ in range(2)]; krs = [kb.sres(st) for _ in range(2)]
                vts = [st.enter_context(nc.sbuf_tensor(un("c_vt"), [128, NKB, 128], BF16)) for _ in range(2)]; vrs = [kb.sres(st) for _ in range(2)]
                qts = [st.enter_context(nc.sbuf_tensor(un("c_qt"), [128, S], BF16)) for _ in range(2)]; qrs = [kb.sres(st) for _ in range(2)]
                oaccs = [st.enter_context(nc.sbuf_tensor(un("c_o"), [128, S], BF16)) for _ in range(2)]; orrs = [kb.sres(st) for _ in range(2)]
                p_r = Ring(kb, st, "c_p", [128, 512], BF16, 8)
                f_r = Ring(kb, st, "c_f", [128, 512], F32, 8)
                sq_r = Ring(kb, st, "c_sq", [128, 512], BF16, 2)
                acc = [(ps_t[i], ps_r[i]) for i in range(4)]
                sring = [(4, 5), (6, 7)]
                sidx = [0]

                def load(h):
                    hb = h % 2
                    load_head(kts[hb], krs[hb], c.r_kc + h, vts[hb], vrs[hb], (c.AKV + c.BH + h) * 128)
                    kb.dma(SP, qts[hb][:], qkT[(c.r_qc + h) * 128:(c.r_qc + h + 1) * 128, :], qrs[hb], writes=[qrs[hb]])

                pairs = [dict(h=h, qi=qi, k=k_, nkb=4 * (qi + 1)) for h in range(c.CH) for qi in range(NQT) for k_ in range(4 * (qi + 1))]

                def stA(p):
                    hb = p["h"] % 2
                    a, b = sring[sidx[0]]
                    sidx[0] = (sidx[0] + 1) % 2
                    s0 = (ps_t[a], ps_r[a]); s1 = (ps_t[b], ps_r[b])
                    p["s"] = (s0, s1)
                    ks = slice(p["k"] * 128, (p["k"] + 1) * 128)
                    qs = slice(p["qi"] * 512, (p["qi"] + 1) * 512)
                    kb.mmg([lambda: pe.matmul(s0[0][:], lhsT=kts[hb][0:64, ks], rhs=qts[hb][0:64, qs], start=True, stop=True),
                            lambda: pe.matmul(s1[0][:], lhsT=kts[hb][64:128, ks], rhs=qts[hb][64:128, qs], start=True, stop=True)],
                           reads=[krs[hb], qrs[hb]], writes=[s0[1], s1[1]])

                def stB(p):
                    ps_ = []
                    for sx in p["s"]:
                        p_, pr = p_r.next()
                        kb.op(ACT, lambda sx=sx, p_=p_: act.activation(out=p_[:], in_=sx[0][:], func=AF.Exp, scale=SC_C),
                              reads=[sx[1]], writes=[pr])
                        if p["k"] >= 4 * p["qi"]:
                            kb.op(POOL, lambda p_=p_: pool.tensor_tensor(out=p_[:], in0=p_[:], in1=masks[:, p["k"] - 4 * p["qi"], :], op=ALU.mult),
                                  reads=[pr, mres], writes=[pr])
                        ps_.append((p_, pr))
                    p["p"] = ps_

                def stC(p):
                    h, qi, k_, nkb = p["h"], p["qi"], p["k"], p["nkb"]
                    hb = h % 2
                    if qi == 0 and k_ == 0 and h + 1 < c.CH:
                        load(h + 1)
                    ps_ = p["p"]
                    qs = slice(qi * 512, (qi + 1) * 512)
                    fl = dict(start=(k_ == 0), stop=(k_ == nkb - 1))
                    kb.mmg([lambda: pe.matmul(acc[0][0][:], lhsT=vts[hb][:, k_, :], rhs=ps_[0][0][:], **fl),
                            lambda: pe.matmul(acc[1][0][:], lhsT=ones_bf[:], rhs=ps_[0][0][:], **fl),
                            lambda: pe.matmul(acc[2][0][:], lhsT=vts[hb][:, k_, :], rhs=ps_[1][0][:], **fl),
                            lambda: pe.matmul(acc[3][0][:], lhsT=ones_bf[:], rhs=ps_[1][0][:], **fl)],
                           reads=[vrs[hb], ps_[0][1], ps_[1][1]], writes=[a_[1] for a_ in acc])
                    if k_ != nkb - 1:
                        return
                    oacc, orr = oaccs[hb], orrs[hb]
                    r0, r0r = f_r.next(); r1, r1r = f_r.next(); o0, o0r = f_r.next(); o1, o1r = f_r.next()
                    kb.op(DVE, lambda: dve.reciprocal(out=r0[:], in_=acc[1][0][:]), reads=[acc[1][1]], writes=[r0r])
                    kb.op(DVE, lambda: dve.reciprocal(out=r1[:], in_=acc[3][0][:]), reads=[acc[3][1]], writes=[r1r])
                    kb.op(DVE, lambda: dve.tensor_tensor(out=o0[:], in0=acc[0][0][:], in1=r0[:], op=ALU.mult), reads=[acc[0][1], r0r], writes=[o0r])
                    kb.op(DVE, lambda: dve.tensor_tensor(out=o1[:], in0=acc[2][0][:], in1=r1[:], op=ALU.mult), reads=[acc[2][1], r1r], writes=[o1r])
                    kb.op(DVE, lambda: dve.scalar_tensor_tensor(out=o0[:], in0=o1[:], scalar=neglam[:, l:l + 1], in1=o0[:],
                                                                op0=ALU.mult, op1=ALU.add), reads=[o1r, o0r, g_res], writes=[o0r])
                    sq_, sqr = sq_r.next()
                    kb.op(POOL, lambda: pool.tensor_tensor(out=sq_[:], in0=o0[:], in1=o0[:], op=ALU.mult), reads=[o0r], writes=[sqr])
                    a, b = sring[sidx[0]]
                    pn = (ps_t[a], ps_r[a])
                    kb.mmg([lambda: pe.matmul(pn[0][:], lhsT=ones_bf[:], rhs=sq_[:], start=True, stop=True)], reads=[sqr], writes=[pn[1]])
                    kb.op(ACT, lambda: act.activation(out=r0[:], in_=pn[0][:], func=AF.Sqrt, bias=float(DIFF_EPS), scale=1.0 / 128.0),
                          reads=[pn[1]], writes=[r0r])
                    kb.op(DVE, lambda: dve.reciprocal(out=r0[:], in_=r0[:]), reads=[r0r], writes=[r0r])
                    kb.op(DVE, lambda: dve.scalar_tensor_tensor(out=oacc[:, qs], in0=o0[:], scalar=dnsc[:, l:l + 1], in1=r0[:],
                                                                op0=ALU.mult, op1=ALU.mult), reads=[o0r, r0r, g_res], writes=[orr])
                    if qi == NQT - 1:
                        row = c.AQ + c.BH + h
                        kb.dma(SP, oT[row * 128:(row + 1) * 128, :], oacc[:], orr, reads=[orr])

                load(0)
                N = len(pairs)
                for i in range(N + 2):
                    if i < N:
                        stA(pairs[i])
                    if 0 <= i - 1 < N:
                        stB(pairs[i - 1])
                    if 0 <= i - 2 < N:
                        stC(pairs[i - 2])
                kb.barrier()

        def attn_b(l):
            with ExitStack() as st:
                masks = st.enter_context(nc.sbuf_tensor(un("mB"), [128, 4, 512], BF16))
                tri = st.enter_context(nc.sbuf_tensor(un("tri"), [128, 128], BF16))
                mres = kb.sres(st)
                kb.dma(POOL, masks[:].rearrange("p a b -> p (a b)"), consts_d[:, c.k_mB:c.k_mB + 2048], mres, writes=[mres])
                kb.dma(POOL, tri[:], consts_d[:, c.k_tri:c.k_tri + 128], mres, writes=[mres])
                kts = [st.enter_context(nc.sbuf_tensor(un("b_kt"), [128, S], BF16)) for _ in range(2)]; krs = [kb.sres(st) for _ in range(2)]
                vts = [st.enter_context(nc.sbuf_tensor(un("b_vt"), [128, NKB, 128], BF16)) for _ in range(2)]; vrs = [kb.sres(st) for _ in range(2)]
                qts = [st.enter_context(nc.sbuf_tensor(un("b_qt"), [128, S], BF16)) for _ in range(2)]; qrs = [kb.sres(st) for _ in range(2)]
                oaccs = [st.enter_context(nc.sbuf_tensor(un("b_o"), [128, S], BF16)) for _ in range(2)]; orrs = [kb.sres(st) for _ in range(2)]
                carry = st.enter_context(nc.sbuf_tensor(un("b_carry"), [128, 512], F32)); cr = kb.res()
                e_r = Ring(kb, st, "b_e", [128, 512], F32, 2)
                sp_r = Ring(kb, st, "b_sp", [128, 512], F32, 6)
                hi_r = Ring(kb, st, "b_hi", [128, 512], BF16, 6)
                lo_r = Ring(kb, st, "b_lo", [128, 512], BF16, 6)
                lb_r = Ring(kb, st, "b_lb", [128, 512], F32, 6)
                t_r = Ring(kb, st, "b_t", [128, 512], F32, 6)
                a_r = Ring(kb, st, "b_a", [128, 512], BF16, 6)
                pso = (ps_t[0], ps_r[0])
                zring = [1, 2, 3]
                zi = [0]
                wring_ = [(4, 5), (6, 7)]
                wi = [0]

                def load(h):
                    hb = h % 2
                    load_head(kts[hb], krs[hb], c.r_kb + h, vts[hb], vrs[hb], (c.AKV + h) * 128)
                    kb.dma(SP, qts[hb][:], qkT[(c.r_qb + h) * 128:(c.r_qb + h + 1) * 128, :], qrs[hb], writes=[qrs[hb]])

                pairs = [dict(h=h, qi=qi, k=k_, n=n_, nkb=4 * (qi + 1)) for h in range(c.BH) for qi in range(NQT)
                         for n_, k_ in enumerate(reversed(range(4 * (qi + 1))))]

                def stA(p):
                    hb = p["h"] % 2
                    zi_ = zring[zi[0]]
                    zi[0] = (zi[0] + 1) % 3
                    pz = (ps_t[zi_], ps_r[zi_])
                    p["z"] = pz
                    ks = slice(p["k"] * 128, (p["k"] + 1) * 128)
                    qs = slice(p["qi"] * 512, (p["qi"] + 1) * 512)
                    kb.mmg([lambda: pe.matmul(pz[0][:], lhsT=kts[hb][:, ks], rhs=qts[hb][:, qs], start=True, stop=True)],
                           reads=[krs[hb], qrs[hb]], writes=[pz[1]])
                    e_, er = e_r.next()
                    kb.op(ACT, lambda: act.activation(out=e_[:], in_=pz[0][:], func=AF.Exp, scale=SC_A), reads=[pz[1]], writes=[er])
                    sp_, spr = sp_r.next()
                    kb.op(ACT, lambda: act.activation(out=sp_[:], in_=e_[:], func=AF.Ln, bias=1.0, scale=1.0), reads=[er], writes=[spr])
                    if p["k"] >= 4 * p["qi"]:
                        kb.op(POOL, lambda: pool.tensor_tensor(out=sp_[:], in0=sp_[:], in1=masks[:, p["k"] - 4 * p["qi"], :], op=ALU.mult),
                              reads=[spr, mres], writes=[spr])
                    p["sp"] = (sp_, spr)

                def stB(p):
                    sp_, spr = p["sp"]
                    pz = p["z"]
                    hi_, hir = hi_r.next()
                    kb.op(POOL, lambda: pool.tensor_copy(out=hi_[:], in_=sp_[:]), reads=[spr], writes=[hir])
                    lb_, lbr = lb_r.next()
                    kb.op(DVE, lambda: dve.scalar_tensor_tensor(out=lb_[:], in0=pz[0][:], scalar=SC_A, in1=sp_[:],
                                                                op0=ALU.mult, op1=ALU.subtract), reads=[pz[1], spr], writes=[lbr])
                    p["hi"] = (hi_, hir); p["lb"] = (lb_, lbr)

                def stB2(p):
                    sp_, spr = p["sp"]
                    hi_, hir = p["hi"]
                    lo_, lor = lo_r.next()
                    kb.op(DVE, lambda: dve.tensor_tensor(out=lo_[:], in0=sp_[:], in1=hi_[:], op=ALU.subtract), reads=[spr, hir], writes=[lor])
                    p["lo"] = (lo_, lor)

                def stC(p):
                    (hi_, hir), (lo_, lor) = p["hi"], p["lo"]
                    a, b = wring_[wi[0]]
                    wi[0] = (wi[0] + 1) % 2
                    pw = (ps_t[a], ps_r[a]); pc = (ps_t[b], ps_r[b])
                    kb.mmg([lambda: pe.matmul(pw[0][:], lhsT=tri[:], rhs=hi_[:], start=True, stop=False),
                            lambda: pe.matmul(pw[0][:], lhsT=tri[:], rhs=lo_[:], start=False, stop=True),
                            lambda: pe.matmul(pc[0][:], lhsT=ones_bf[:], rhs=hi_[:], start=True, stop=False),
                            lambda: pe.matmul(pc[0][:], lhsT=ones_bf[:], rhs=lo_[:], start=False, stop=True)],
                           reads=[hir, lor, mres], writes=[pw[1], pc[1]])
                    if p["n"] == 0:
                        kb.op(POOL, lambda: pool.memset(carry[:], 0.0), writes=[cr])
                    t_, tr = t_r.next()
                    kb.op(DVE, lambda: dve.tensor_tensor(out=t_[:], in0=pw[0][:], in1=carry[:], op=ALU.add), reads=[pw[1], cr], writes=[tr])
                    kb.op(DVE, lambda: dve.tensor_tensor(out=carry[:], in0=pc[0][:], in1=carry[:], op=ALU.add), reads=[pc[1], cr], writes=[cr])
                    p["t"] = (t_, tr)

                def stC2(p):
                    t_, tr = p["t"]
                    lb_, lbr = p["lb"]
                    kb.op(POOL, lambda: pool.tensor_tensor(out=t_[:], in0=lb_[:], in1=t_[:], op=ALU.subtract), reads=[lbr, tr], writes=[tr])

                def stD(p):
                    t_, tr = p["t"]
                    a_, ar = a_r.next()
                    kb.op(ACT, lambda: act.activation(out=a_[:], in_=t_[:], func=AF.Exp), reads=[tr], writes=[ar])
                    if p["k"] >= 4 * p["qi"]:
                        kb.op(POOL, lambda: pool.tensor_tensor(out=a_[:], in0=a_[:], in1=masks[:, p["k"] - 4 * p["qi"], :], op=ALU.mult),
                              reads=[ar, mres], writes=[ar])
                    p["a"] = (a_, ar)

                def stE(p):
                    h, qi, k_, n_, nkb = p["h"], p["qi"], p["k"], p["n"], p["nkb"]
                    hb = h % 2
                    if qi == 0 and n_ == 0 and h + 1 < c.BH:
                        load(h + 1)
                    a_, ar = p["a"]
                    qs = slice(qi * 512, (qi + 1) * 512)
                    kb.mmg([lambda: pe.matmul(pso[0][:], lhsT=vts[hb][:, k_, :], rhs=a_[:], start=(n_ == 0), stop=(n_ == nkb - 1))],
                           reads=[vrs[hb], ar], writes=[pso[1]])
                    if n_ == nkb - 1:
                        kb.op(ACT, lambda: act.copy(out=oaccs[hb][:, qs], in_=pso[0][:]), reads=[pso[1]], writes=[orrs[hb]])
                        if qi == NQT - 1:
                            row = c.AQ + h
                            kb.dma(SP, oT[row * 128:(row + 1) * 128, :], oaccs[hb][:], orrs[hb], reads=[orrs[hb]])
                    for key in ("z", "sp", "hi", "lo", "lb", "t", "a"):
                        p.pop(key, None)

                load(0)
                N = len(pairs)
                stages = (stA, stB, stB2, stC, stC2, stD, stE)
                for i in range(N + 6):
                    for si, fn in enumerate(stages):
                        if 0 <= i - si < N:
                            fn(pairs[i - si])
                kb.barrier()

        def mixer_out_phase(l):
            par = l % 2
            NOC = c.NOC
            br = [("w_branch_a", 0, c.AQ), ("w_branch_b", c.AQ, c.BH), ("w_branch_c", c.AQ + c.BH, c.CH)]
            for p in range(S // TP):
                t0 = p * TP
                with ExitStack() as st:
                    ot = st.enter_context(nc.sbuf_tensor(un("ot"), [128, NOC, TP], BF16)); otr = kb.sres(st)
                    mT = st.enter_context(nc.sbuf_tensor(un("mT"), [128, DC, TP], BF16)); mres_ = kb.res()
                    wring = Ring(kb, st, "wr", [128, 32, 128], BF16, 4, dma=True)
                    for ch in range(NOC):
                        kb.dma(SP, ot[:, ch, :], oT[ch * 128:(ch + 1) * 128, t0:t0 + TP], otr, writes=[otr])
                    with ExitStack() as st2:
                        gring = Ring(kb, st2, "gr", [128, 3, TP], BF16, 3 if NT >= 2 else 4, dma=True)
                        tmp = Ring(kb, st2, "mt", [128, 512], F32, 6)
                        units = []
                        for m in range(DC):
                            stt = {}
                            for t in range(NT):

                                def load(m=m, t=t, stt=stt):
                                    if t == 0:
                                        stt["w"] = wring.next()
                                        for (wn, k0, kc) in br:
                                            load_w(par, wn, m, stt["w"][0], stt["w"][1], k0=0, dstk0=k0, KC=kc)
                                        stt["g"] = gring.next()
                                        for b_ in range(3):
                                            kb.dma(SP, stt["g"][0][:, b_, :], gT[b_ * D + m * 128:b_ * D + (m + 1) * 128, t0:t0 + TP],
                                                   stt["g"][1], writes=[stt["g"][1]])

                                def comp(m=m, t=t, stt=stt):
                                    wtile, wr = stt["w"]
                                    pss = [ps_next() for _ in range(3)]
                                    stt["ps", t] = pss
                                    fns = []
                                    for b_, (wn, k0, kc) in enumerate(br):
                                        for k in range(kc):
                                            fns.append(lambda b_=b_, k=k, k0=k0, kc=kc: pe.matmul(
                                                pss[b_][0][:], lhsT=wtile[:, k0 + k, :], rhs=ot[:, k0 + k, t * 512:(t + 1) * 512],
                                                start=(k == 0), stop=(k == kc - 1)))
                                    kb.mmg(fns, reads=[wr, otr], writes=[p_[1] for p_ in pss])

                                def epi(m=m, t=t, stt=stt):
                                    gt_, gr_ = stt["g"]
                                    pss = stt["ps", t]
                                    ts_ = [tmp.next() for _ in range(3)]
                                    for b_ in range(3):
                                        kb.op(DVE, lambda b_=b_: dve.tensor_tensor(out=ts_[b_][0][:], in0=pss[b_][0][:],
                                                                                   in1=gt_[:, b_, t * 512:(t + 1) * 512], op=ALU.mult),
                                              reads=[pss[b_][1], gr_], writes=[ts_[b_][1]])
                                    kb.op(POOL, lambda: pool.tensor_tensor(out=ts_[0][0][:], in0=ts_[0][0][:], in1=ts_[1][0][:], op=ALU.add),
                                          reads=[ts_[0][1], ts_[1][1]], writes=[ts_[0][1]])
                                    kb.op(POOL, lambda: pool.tensor_tensor(out=mT[:, m, t * 512:(t + 1) * 512], in0=ts_[0][0][:], in1=ts_[2][0][:], op=ALU.add),
                                          reads=[ts_[0][1], ts_[2][1]], writes=[mres_])

                                units.append((load, comp, epi))
                        run_units(units, 2)
                        kb.barrier()
                    with ExitStack() as st2:
                        gemm_out_residual(st2, par, "w_out", DC, mT, mres_, t0, 1.0, wring)
                        kb.barrier()
                kb.barrier()

        def final_phase():
            TPF = 512
            for p in range(S // TPF):
                t0 = p * TPF
                with ExitStack() as st:
                    hF = st.enter_context(nc.sbuf_tensor(un("hF"), [128, DC, TPF], F32)); hres = kb.res()
                    ident = st.enter_context(nc.sbuf_tensor(un("identf"), [128, 128], F32)); ir = kb.sres(st)
                    kb.dma(SP, ident[:], consts_d[:, c.k_ident:c.k_ident + 128], ir, writes=[ir])
                    with ExitStack() as st2:
                        norm_pass(st2, t0, TPF, c.v_nf, hF, hres, EPS)
                        kb.barrier()
                    oring = Ring(kb, st, "fo", [128, D], F32, 2, dma=True)
                    for jt in range(TPF // 128):
                        ot_, orr = oring.next()
                        for c0 in range(0, DC, 4):
                            pst, psr = ps_next()
                            nn = min(4, DC - c0)
                            kb.mmg([lambda a=a: pe.transpose(out=pst[:, a * 128:(a + 1) * 128],
                                                             in_=hF[:, c0 + a, jt * 128:(jt + 1) * 128], identity=ident[:])
                                    for a in range(nn)], reads=[hres, ir], writes=[psr])
                            if (c0 // 4) % 2 == 0:
                                kb.op(DVE, lambda: dve.tensor_copy(out=ot_[:, c0 * 128:(c0 + nn) * 128], in_=pst[:, 0:nn * 128]), reads=[psr], writes=[orr])
                            else:
                                kb.op(ACT, lambda: act.copy(out=ot_[:, c0 * 128:(c0 + nn) * 128], in_=pst[:, 0:nn * 128]), reads=[psr], writes=[orr])
                        kb.dma(SP, out_d[t0 + jt * 128:t0 + (jt + 1) * 128, :], ot_[:], orr, reads=[orr])
                kb.barrier()

        stop = dbg or "none"
        for l in range(NL):
            if l + 1 < NL and not dbg:
                queue_conv(l + 1)
            ffn_phase(l, c.v_n1 + l * DC, "ffn1_w_in", "ffn1_w_out")
            if stop == "ffn1":
                break
            mixer_in_phase(l)
            attn_a(l)
            attn_b(l)
            attn_c(l)
            kb.barrier()
            if stop == "attn":
                break
            mixer_out_phase(l)
            if stop == "mixer":
                break
            ffn_phase(l, c.v_n2 + l * DC, "ffn2_w_in", "ffn2_w_out")
            if stop == "ffn2":
                break
            conv_pump(len(conv_q))
        if not dbg:
            final_phase()
        kb.barrier()
        for E in (PE, ACT, DVE, POOL):
            pass
    kb.top.close()
    return nc


def rope_tab(S, dim):
    inv = (1.0 / (np.float32(ROPE_THETA) ** (np.arange(0, dim, 2, dtype=np.float32) / np.float32(dim)))).astype(np.float32)
    ang = (np.arange(S, dtype=np.float32)[:, None] * inv[None, :]).astype(np.float32)
    return np.cos(ang).astype(np.float32), np.sin(ang).astype(np.float32)


def make_consts(c):
    S = c.S
    K = np.zeros((128, c.NCC), np.float32)
    K[:, c.k_ones:c.k_ones + 128] = 1.0
    K[:, c.k_ident:c.k_ident + 128] = np.eye(128, dtype=np.float32)
    k = np.arange(128)[:, None]
    m = np.arange(128)[None, :]
    K[:, c.k_permA:c.k_permA + 128] = (k == (m + 64) % 128)
    K[:, c.k_permC:c.k_permC + 128] = (k == 64 * (m // 64) + ((m % 64) + 32) % 64)
    K[:, c.k_tri:c.k_tri + 128] = (k > m)
    md = (k <= m).astype(np.float32)
    mp = (k > m).astype(np.float32)
    K[:, c.k_mAd:c.k_mAd + c.G * 128] = np.tile(md, (1, c.G))
    K[:, c.k_mAp:c.k_mAp + c.G * 128] = np.tile(mp, (1, c.G))
    q = np.arange(512)[None, :]
    for a in range(4):
        K[:, c.k_mC + a * 512:c.k_mC + (a + 1) * 512] = (k + a * 128 <= q)
        K[:, c.k_mB + a * 512:c.k_mB + (a + 1) * 512] = (k + a * 128 < q)
    rope = np.zeros((128, 4 * S), np.float32)
    ca, sa = rope_tab(S, 128)
    d = np.arange(128)
    rope[:, 0:S] = ca.T[d % 64, :]
    rope[:, S:2 * S] = sa.T[d % 64, :] * np.where(d < 64, -1.0, 1.0)[:, None].astype(np.float32)
    cc, sc = rope_tab(S, 64)
    rope[:, 2 * S:3 * S] = cc.T[(d % 64) % 32, :]
    rope[:, 3 * S:4 * S] = sc.T[(d % 64) % 32, :] * np.where((d % 64) < 32, -1.0, 1.0)[:, None].astype(np.float32)
    return K, rope


def make_vecs(c, inp):
    NL, DC = c.NL, c.DC
    V = np.zeros((128, c.NV), np.float32)

    def colmajor(a):
        a = np.asarray(a, np.float32).reshape(-1, DC, 128)
        return a.transpose(2, 0, 1).reshape(128, -1)

    V[:, c.v_n1:c.v_n1 + NL * DC] = colmajor(inp["ffn1_norm"])
    V[:, c.v_nm:c.v_nm + NL * DC] = colmajor(inp["mix_norm"])
    V[:, c.v_n2:c.v_n2 + NL * DC] = colmajor(inp["ffn2_norm"])
    V[:, c.v_nf:c.v_nf + DC] = colmajor(np.asarray(inp["final_norm"]).reshape(1, -1))
    V[:, c.v_sink:c.v_sink + NL * c.AQ] = np.asarray(inp["sinks"], np.float32).reshape(1, -1)
    for nm, off in (("lambda_q1", c.v_lq1), ("lambda_k1", c.v_lk1), ("lambda_q2", c.v_lq2), ("lambda_k2", c.v_lk2)):
        V[:, off:off + NL * 64] = np.asarray(inp[nm], np.float32).reshape(1, -1)
    V[:, c.v_dn:c.v_dn + NL] = np.asarray(inp["diff_norm"], np.float32).T
    return V


def run(cfg, inputs, dbg=None, trace=False):
    c = cfg
    nc = build_program(c, dbg)
    K, rope = make_consts(c)
    V = make_vecs(c, inputs)
    x = np.asarray(inputs["x"], np.float32)
    wflat = {f"{n}_{l}": np.ascontiguousarray(np.asarray(inputs[n], np.float32)[l]) for n, Kd, N in c.wspecs() for l in range(c.NL)}
    B = x.shape[0]
    outs = []
    for b0 in range(0, B, c.NW):
        nw = min(c.NW, B - b0)
        in_maps = []
        for w in range(nw):
            m = {"x": np.ascontiguousarray(x[b0 + w]), "vecs": V, "consts": K, "rope": rope}
            m.update(wflat)
            in_maps.append(m)
        res = run_bass_kernel_spmd(nc, in_maps, core_ids=list(range(nw)), **({"trace": True} if trace else {}))
        if trace:
            print("EXEC_NS", res.exec_time_ns)
        if dbg:
            return res.results
        outs += [res.results[w]["out"] for w in range(nw)]
    return np.stack(outs, axis=0).astype(np.float32)


def kernel(**inputs):
    return run(Cfg(), inputs)
```
